# Optimizing a Trainium2 kernel written in Bass

```python
import jax, jax.numpy as jnp
from jax import lax
import numpy as np

D_MODEL = 1024
BATCH = 8
SEQ = 2048
DEPTH = 1

CHUNK = 64
D_FF = 2816
NORM_EPS = 1e-6

RWKV_HEADS = 8
RWKV_HEAD_DIM = 64
D_RWKV = RWKV_HEADS * RWKV_HEAD_DIM
DECAY_LORA = 64
ICLR_LORA = 64
GATE_LORA = 128
RWKV_GN_EPS = 64e-5

SSM_HEADS = 8
SSM_HEAD_DIM = 64
D_SSM = SSM_HEADS * SSM_HEAD_DIM
SSM_GROUPS = 2
SSM_STATE = 128
CONV_WIDTH = 4
D_CONV = D_SSM + 2 * SSM_GROUPS * SSM_STATE
SSM_NORM_EPS = 1e-5

D_MIX = D_RWKV + D_SSM
D_IN_RWKV = 3 * D_RWKV + DECAY_LORA + ICLR_LORA + GATE_LORA
D_IN_SSM = D_SSM + D_CONV + SSM_HEADS
D_IN = D_IN_RWKV + D_IN_SSM

kernel_name = "hybrid_rwkv7_mamba2_macaron_block"


def rms_norm(x, g, eps=NORM_EPS):
    x32 = x.astype(jnp.float32)
    y = x32 * lax.rsqrt(jnp.mean(x32 * x32, axis=-1, keepdims=True) + eps)
    return y.astype(x.dtype) * g


def swiglu_ffn(h, w_gate, w_up, w_down):
    return (jax.nn.silu(h @ w_gate) * (h @ w_up)) @ w_down


def token_shift(t):
    return jnp.pad(t, ((0, 0), (1, 0), (0, 0)))[:, :-1]


def rwkv7_mix(p, mu, w0, w2, a0, a2, g2, k_k, k_a, r_k, gn_g, gn_b):
    b, L, _ = p.shape
    H, N = RWKV_HEADS, RWKV_HEAD_DIM
    p = p + (token_shift(p) - p) * mu
    r, k, v, wd, ad, gd = jnp.split(
        p, [D_RWKV, 2 * D_RWKV, 3 * D_RWKV, 3 * D_RWKV + DECAY_LORA,
            3 * D_RWKV + DECAY_LORA + ICLR_LORA], axis=-1)
    w_log = -jax.nn.softplus(-(w0 + jnp.tanh(wd) @ w2)) - 0.5
    decay = jnp.exp(-jnp.exp(w_log))
    a = jax.nn.sigmoid(a0 + ad @ a2)
    g = jax.nn.sigmoid(gd) @ g2
    heads = lambda t: t.reshape(b, L, H, N)
    kk = heads(k * k_k)
    kk32 = kk.astype(jnp.float32)
    kk = (kk32 / jnp.maximum(jnp.sqrt(jnp.sum(kk32 * kk32, -1, keepdims=True)), 1e-12)).astype(k.dtype)
    k = k * (1.0 + (a - 1.0) * k_a)
    r_h, w_h, k_h, v_h, a_h = heads(r), heads(decay), heads(k), heads(v), heads(a)

    def step(S, inp):
        rt, wt, kt, vt, kkt, at = inp
        sa = jnp.einsum('bhij,bhj->bhi', S, -kkt)
        S = S * wt[:, :, None, :] + sa[..., None] * (kkt * at)[:, :, None, :] + vt[..., None] * kt[:, :, None, :]
        return S, jnp.einsum('bhij,bhj->bhi', S, rt)

    xs = tuple(jnp.moveaxis(t, 1, 0) for t in (r_h, w_h, k_h, v_h, kk, a_h))
    S0 = jnp.zeros((b, H, N, N), r.dtype)
    _, ys = lax.scan(step, S0, xs)
    y = jnp.moveaxis(ys, 0, 1)
    y32 = y.astype(jnp.float32)
    mean = jnp.mean(y32, -1, keepdims=True)
    var = jnp.mean(jnp.square(y32 - mean), -1, keepdims=True)
    y = ((y32 - mean) * lax.rsqrt(var + RWKV_GN_EPS)).astype(y.dtype).reshape(b, L, D_RWKV) * gn_g + gn_b
    bonus = (jnp.sum(r_h * k_h * r_k, -1, keepdims=True) * v_h).reshape(b, L, D_RWKV)
    return (y + bonus) * g


def causal_depthwise_conv(t, w):
    C = t.shape[-1]
    return lax.conv_general_dilated(
        t, w[:, None, :], window_strides=(1,), padding=[(CONV_WIDTH - 1, 0)],
        dimension_numbers=('NWC', 'WIO', 'NWC'), feature_group_count=C)


def ssd_scan(x, a, Bm, Cm):
    b, L, H, P = x.shape
    G, N = Bm.shape[2], Bm.shape[3]
    R = H // G
    nc = L // CHUNK
    x = x.reshape(b, nc, CHUNK, G, R, P)
    a = a.reshape(b, nc, CHUNK, G, R).transpose(0, 3, 4, 1, 2)
    Bc = Bm.reshape(b, nc, CHUNK, G, N)
    Cc = Cm.reshape(b, nc, CHUNK, G, N)
    a_cs = jnp.cumsum(a, axis=-1)
    causal = jnp.tril(jnp.ones((CHUNK, CHUNK), dtype=bool))
    seg = a_cs[..., :, None] - a_cs[..., None, :]
    L_mat = jnp.exp(jnp.where(causal, seg, -jnp.inf))
    scores = jnp.einsum('bclgn,bcsgn->bgcls', Cc, Bc)
    y_diag = jnp.einsum('bgcls,bgrcls,bcsgrp->bclgrp', scores, L_mat, x)
    decay_to_end = jnp.exp(a_cs[..., -1:] - a_cs)
    chunk_states = jnp.einsum('bclgn,bgrcl,bclgrp->bcgrpn', Bc, decay_to_end, x)
    chunk_decay = jnp.exp(a_cs[..., -1])

    def carry_fn(state, inp):
        st, dec = inp
        return state * dec[..., None, None] + st, state

    init = jnp.zeros((b, G, R, P, N), x.dtype)
    _, prev_states = lax.scan(carry_fn, init,
                              (jnp.moveaxis(chunk_states, 1, 0), jnp.moveaxis(chunk_decay, 3, 0)))
    y_off = jnp.einsum('bclgn,bgrcl,cbgrpn->bclgrp', Cc, jnp.exp(a_cs), prev_states)
    return (y_diag + y_off).reshape(b, L, H, P)


def mamba2_mix(p, conv_w, conv_b, dt_bias, a_log, d_skip, norm_g):
    b, L, _ = p.shape
    z, xbc, dt = jnp.split(p, [D_SSM, D_SSM + D_CONV], axis=-1)
    xbc = jax.nn.silu(causal_depthwise_conv(xbc, conv_w) + conv_b)
    xs, Bm, Cm = jnp.split(xbc, [D_SSM, D_SSM + SSM_GROUPS * SSM_STATE], axis=-1)
    dt = jax.nn.softplus(dt + dt_bias)
    A = -jnp.exp(a_log)
    xh = xs.reshape(b, L, SSM_HEADS, SSM_HEAD_DIM)
    y = ssd_scan(xh * dt[..., None], dt * A,
                 Bm.reshape(b, L, SSM_GROUPS, SSM_STATE), Cm.reshape(b, L, SSM_GROUPS, SSM_STATE))
    y = (y + d_skip[:, None] * xh).reshape(b, L, D_SSM)
    return rms_norm(y * jax.nn.silu(z), norm_g, SSM_NORM_EPS)


def setup_inputs(seed: int = 0) -> dict:
    key = jax.random.key(seed)
    ks = jax.random.split(key, 32)
    f32 = jnp.float32
    nrm = lambda k, shape, s: jax.random.normal(k, shape, f32) * s
    gain = lambda k, shape: 1.0 + 0.02 * jax.random.normal(k, shape, f32)
    Dd = DEPTH
    dt0 = jnp.exp(jax.random.uniform(ks[20], (Dd, SSM_HEADS), f32, np.log(1e-3), np.log(1e-1)))
    return {
        "x": nrm(ks[0], (BATCH, SEQ, D_MODEL), 1.0),
        "norm_ffn1": gain(ks[1], (Dd, D_MODEL)),
        "ffn1_w_gate": nrm(ks[2], (Dd, D_MODEL, D_FF), D_MODEL ** -0.5),
        "ffn1_w_up": nrm(ks[3], (Dd, D_MODEL, D_FF), D_MODEL ** -0.5),
        "ffn1_w_down": nrm(ks[4], (Dd, D_FF, D_MODEL), D_FF ** -0.5),
        "norm_mix": gain(ks[5], (Dd, D_MODEL)),
        "w_in": nrm(ks[6], (Dd, D_MODEL, D_IN), D_MODEL ** -0.5),
        "rwkv_mu": jax.random.uniform(ks[7], (Dd, D_IN_RWKV), f32),
        "rwkv_w0": jax.random.uniform(ks[8], (Dd, D_RWKV), f32, -6.0, 1.0),
        "rwkv_w2": nrm(ks[9], (Dd, DECAY_LORA, D_RWKV), 0.5 * DECAY_LORA ** -0.5),
        "rwkv_a0": nrm(ks[10], (Dd, D_RWKV), 0.1),
        "rwkv_a2": nrm(ks[11], (Dd, ICLR_LORA, D_RWKV), ICLR_LORA ** -0.5),
        "rwkv_g2": nrm(ks[12], (Dd, GATE_LORA, D_RWKV), GATE_LORA ** -0.5),
        "rwkv_k_k": 0.85 + 0.02 * jax.random.normal(ks[13], (Dd, D_RWKV), f32),
        "rwkv_k_a": gain(ks[14], (Dd, D_RWKV)),
        "rwkv_r_k": nrm(ks[15], (Dd, RWKV_HEADS, RWKV_HEAD_DIM), 0.1),
        "rwkv_gn_g": gain(ks[16], (Dd, D_RWKV)),
        "rwkv_gn_b": nrm(ks[17], (Dd, D_RWKV), 0.01),
        "ssm_conv_w": nrm(ks[18], (Dd, CONV_WIDTH, D_CONV), CONV_WIDTH ** -0.5),
        "ssm_conv_b": nrm(ks[19], (Dd, D_CONV), 0.01),
        "ssm_dt_bias": dt0 + jnp.log(-jnp.expm1(-dt0)),
        "ssm_a_log": jnp.log(jax.random.uniform(ks[21], (Dd, SSM_HEADS), f32, 1.0, 16.0)),
        "ssm_d": gain(ks[22], (Dd, SSM_HEADS)),
        "ssm_norm": gain(ks[23], (Dd, D_SSM)),
        "w_out": nrm(ks[24], (Dd, D_MIX, D_MODEL), D_MIX ** -0.5),
        "norm_ffn2": gain(ks[25], (Dd, D_MODEL)),
        "ffn2_w_gate": nrm(ks[26], (Dd, D_MODEL, D_FF), D_MODEL ** -0.5),
        "ffn2_w_up": nrm(ks[27], (Dd, D_MODEL, D_FF), D_MODEL ** -0.5),
        "ffn2_w_down": nrm(ks[28], (Dd, D_FF, D_MODEL), D_FF ** -0.5),
        "norm_final": gain(ks[29], (D_MODEL,)),
    }


def reference(x, norm_ffn1, ffn1_w_gate, ffn1_w_up, ffn1_w_down, norm_mix, w_in,
              rwkv_mu, rwkv_w0, rwkv_w2, rwkv_a0, rwkv_a2, rwkv_g2, rwkv_k_k, rwkv_k_a,
              rwkv_r_k, rwkv_gn_g, rwkv_gn_b, ssm_conv_w, ssm_conv_b, ssm_dt_bias,
              ssm_a_log, ssm_d, ssm_norm, w_out, norm_ffn2, ffn2_w_gate, ffn2_w_up,
              ffn2_w_down, norm_final):
    for i in range(DEPTH):
        h = rms_norm(x, norm_ffn1[i])
        x = x + 0.5 * swiglu_ffn(h, ffn1_w_gate[i], ffn1_w_up[i], ffn1_w_down[i])
        h = rms_norm(x, norm_mix[i])
        p = h @ w_in[i]
        y_rwkv = rwkv7_mix(p[..., :D_IN_RWKV], rwkv_mu[i], rwkv_w0[i], rwkv_w2[i], rwkv_a0[i],
                           rwkv_a2[i], rwkv_g2[i], rwkv_k_k[i], rwkv_k_a[i], rwkv_r_k[i],
                           rwkv_gn_g[i], rwkv_gn_b[i])
        y_ssm = mamba2_mix(p[..., D_IN_RWKV:], ssm_conv_w[i], ssm_conv_b[i], ssm_dt_bias[i],
                           ssm_a_log[i], ssm_d[i], ssm_norm[i])
        x = x + jnp.concatenate([y_rwkv, y_ssm], axis=-1) @ w_out[i]
        h = rms_norm(x, norm_ffn2[i])
        x = x + 0.5 * swiglu_ffn(h, ffn2_w_gate[i], ffn2_w_up[i], ffn2_w_down[i])
    return rms_norm(x, norm_final)
```

```python
import os
import numpy as np
from contextlib import ExitStack
import concourse.bass as bass
import concourse.mybir as mybir
from concourse.bass_utils import run_bass_kernel_spmd

F32 = mybir.dt.float32
BF16 = mybir.dt.bfloat16
AF = mybir.ActivationFunctionType
ALU = mybir.AluOpType

D = 1024
L = 2048
DFF = 2816
NFC = DFF // 128
D_IN = 3336
NCORES = 8
DECAY_C = 0.6065306597126334
MODEL_LAT = float(os.environ.get("MK_LAT", "0.4"))
MODEL_PE = float(os.environ.get("MK_PE", "0.9"))
MODEL_EW = float(os.environ.get("MK_EW", "0.7"))
MODEL_RT = float(os.environ.get("MK_RT", "0.6"))


class Buf:
    __slots__ = ("name", "w", "rs")

    def __init__(self, name):
        self.name = name
        self.w = None
        self.rs = []


class Sched:
    ENGS = ("pe", "act", "dve", "pool", "sp")
    SAME_ENGINE_RAW = {"act": True, "dve": True, "pool": True, "pe": False, "sp": False}

    def __init__(self):
        self.ops = {e: [] for e in self.ENGS}
        self.cnt = {}
        self.known = {e: {} for e in self.ENGS}
        self.selfw = {e: 0 for e in self.ENGS}
        self.semkeys = []
        self.nwait = 0
        self.ninst = 0
        self.t_eng = {e: 0.0 for e in self.ENGS}
        self.t_ev = {}
        self.stream = None
        self.head = {}

    def _sem(self, key):
        if key not in self.cnt:
            self.cnt[key] = 0
            self.semkeys.append(key)

    def _merge(self, eng, ev):
        kn = self.known[eng]
        for k, v in ev[2].items():
            if kn.get(k, 0) < v:
                kn[k] = v
        if kn.get(ev[0], 0) < ev[1]:
            kn[ev[0]] = ev[1]

    def emit(self, eng, fn, reads=(), writes=(), dma=None, ninc=1, small=False, force=(), cost=None):
        deps = []
        for b in reads:
            if b.w is not None:
                deps.append((b.w, True))
        for b in writes:
            if b.w is not None:
                deps.append((b.w, False))
            for r in b.rs:
                deps.append((r, False))
        kn = self.known[eng]
        wmax = {}
        for ev, raw in deps:
            k, v = ev[0], ev[1]
            if k == eng:
                if not (raw and self.SAME_ENGINE_RAW[eng] and (small or ev[3])) or self.selfw[eng] >= v:
                    continue
                self.selfw[eng] = v
                if wmax.get(k, 0) < v:
                    wmax[k] = v
                continue
            if kn.get(k, 0) >= v:
                continue
            if wmax.get(k, 0) < v:
                wmax[k] = v
            self._merge(eng, ev)
        for ev in force:
            if wmax.get(ev[0], 0) < ev[1]:
                wmax[ev[0]] = ev[1]
        waits = list(wmax.items())
        self.nwait += len(waits)
        self.ninst += 1
        if dma is not None:
            key, step = dma, 16
        else:
            key, step = eng, 1
        self._sem(key)
        self.cnt[key] += step * ninc
        val = self.cnt[key]
        ev = (key, val, dict(kn), small)
        ready = 0.0
        for ev_d, _raw in deps:
            t_ = self.t_ev.get((ev_d[0], ev_d[1]), 0.0)
            if t_ > ready:
                ready = t_
        for ev_d in force:
            t_ = self.t_ev.get((ev_d[0], ev_d[1]), 0.0)
            if t_ > ready:
                ready = t_
        if cost is None:
            cost = 2.5 if dma is not None else (0.25 if small else (MODEL_PE if eng == "pe" else MODEL_EW))
        start = max(self.t_eng[eng], ready + MODEL_LAT)
        if dma is not None:
            self.t_eng[eng] = start + 0.1
            end = start + cost
        else:
            end = start + cost
            self.t_eng[eng] = end
        self.t_ev[(key, val)] = end
        if self.stream is not None:
            if self.head.get(self.stream, 0.0) < end:
                self.head[self.stream] = end
        if dma is None:
            kn[key] = val

        def run(e, sems, waits=waits, fn=fn, key=key, step=step):
            for k, v in waits:
                e.wait_ge(sems[k], v)
            ins = fn(e)
            if isinstance(ins, (list, tuple)):
                for i_ in ins:
                    i_.then_inc(sems[key], step)
            else:
                ins.then_inc(sems[key], step)

        self.ops[eng].append(run)
        for b in writes:
            b.w = ev
            b.rs = []
        for b in reads:
            if b not in writes:
                b.rs.append(ev)
        return ev

    def wait_all(self, eng, bufs):
        waits = {}
        for b in bufs:
            if b.w is not None:
                k, v = b.w[0], b.w[1]
                if waits.get(k, 0) < v:
                    waits[k] = v

        def run(e, sems, waits=waits):
            for k, v in waits.items():
                e.wait_ge(sems[k], v)
        self.ops[eng].append(run)

    def build(self, nc, stack):
        sems = {}
        for i, k in enumerate(self.semkeys):
            sems[k] = stack.enter_context(nc.semaphore("s%d" % i))
        block = stack.enter_context(nc.Block())
        ops = self.ops

        @block.tensor
        def _(e):
            for f in ops["pe"]:
                f(e, sems)

        @block.scalar
        def _(e):
            for f in ops["act"]:
                f(e, sems)

        @block.vector
        def _(e):
            for f in ops["dve"]:
                f(e, sems)

        @block.gpsimd
        def _(e):
            for f in ops["pool"]:
                f(e, sems)

        @block.sync
        def _(e):
            for f in ops["sp"]:
                f(e, sems)


def _cols(v):
    v = np.asarray(v, np.float32).reshape(-1, 128)
    return np.ascontiguousarray(v.T)


PP_LAYOUT = {}


def _pack_pp(inp):
    parts = [
        ("g1", _cols(inp["norm_ffn1"][0])),
        ("gm", _cols(inp["norm_mix"][0])),
        ("g2", _cols(inp["norm_ffn2"][0])),
        ("gf", _cols(inp["norm_final"])),
        ("mu", _cols(inp["rwkv_mu"][0])),
        ("w0", _cols(inp["rwkv_w0"][0])),
        ("a0", _cols(inp["rwkv_a0"][0])),
        ("kk", _cols(inp["rwkv_k_k"][0])),
        ("ka", _cols(inp["rwkv_k_a"][0])),
        ("rk", _cols(inp["rwkv_r_k"][0].reshape(-1))),
        ("gng", _cols(inp["rwkv_gn_g"][0])),
        ("gnb", _cols(inp["rwkv_gn_b"][0])),
        ("cw0", _cols(inp["ssm_conv_w"][0][0])),
        ("cw1", _cols(inp["ssm_conv_w"][0][1])),
        ("cw2", _cols(inp["ssm_conv_w"][0][2])),
        ("cw3", _cols(inp["ssm_conv_w"][0][3])),
        ("cb", _cols(inp["ssm_conv_b"][0])),
    ]
    off = 0
    for k, a in parts:
        PP_LAYOUT[k] = off
        off += a.shape[1]
    return np.ascontiguousarray(np.concatenate([a for _, a in parts], axis=1))


PR_LAYOUT = {}


def _pack_pr(inp):
    rep = lambda v: np.ascontiguousarray(np.broadcast_to(np.asarray(v, np.float32).reshape(1, -1), (128, np.asarray(v).size)))
    parts = [
        ("dtb", rep(inp["ssm_dt_bias"][0])),
        ("alog", rep(inp["ssm_a_log"][0])),
        ("dsk", rep(inp["ssm_d"][0])),
        ("sng", rep(inp["ssm_norm"][0])),
    ]
    off = 0
    for k, a in parts:
        PR_LAYOUT[k] = off
        off += a.shape[1]
    return np.ascontiguousarray(np.concatenate([a for _, a in parts], axis=1))


CONST_LAYOUT = {}


def _consts():
    p = np.arange(128)
    e = p // 64
    j = p % 64
    i64 = np.arange(64)
    su = (j[:, None] < i64[None, :]).astype(np.float32)
    iu = (j[:, None] <= i64[None, :]).astype(np.float32)
    sl = (j[:, None] > i64[None, :]).astype(np.float32)
    eye = (j[:, None] == i64[None, :]).astype(np.float32)
    t512 = np.arange(512)
    parts = [
        ("iu", np.tile(iu, (1, 8))),
        ("blk", (e[:, None] == e[None, :]).astype(np.float32)),
        ("tri", ((e[:, None] == e[None, :]) & (j[:, None] <= j[None, :])).astype(np.float32)),
        ("sel0", np.broadcast_to((e[:, None] == 0), (128, 128)).astype(np.float32)),
        ("sel1", np.broadcast_to((e[:, None] == 1), (128, 128)).astype(np.float32)),
        ("rst", np.broadcast_to((t512 % 64 != 0)[None, :], (128, 512)).astype(np.float32)),
        ("ident", np.eye(128, dtype=np.float32)),
        ("su", np.tile(su, (1, 8))),
        ("sl", np.tile(sl, (1, 8))),
        ("eye", np.tile(eye, (1, 8))),
        ("ones", np.ones((128, 128), np.float32)),
    ]
    off = 0
    for k, a in parts:
        CONST_LAYOUT[k] = off
        off += a.shape[1]
    return np.ascontiguousarray(np.concatenate([a for _, a in parts], axis=1))


class Arena:
    def __init__(self, t, nbytes):
        self.t = t
        self.n = nbytes
        self.off = 0
        self.peak = 0
        self.live = []
        self.pending = {}

    def alloc(self, shape, dt=F32):
        nel = 1
        for d_ in shape[1:]:
            nel *= d_
        esz = 4 if dt == F32 else 2
        nb = (nel * esz + 63) // 64 * 64
        o = self.off
        self.off += nb
        self.peak = max(self.peak, self.off)
        assert self.off <= self.n, "arena overflow: %d > %d" % (self.off, self.n)
        v = self.t[:, o // 4:(o + nb) // 4]
        if dt != F32:
            v = v.bitcast(dt)
        v = v[:, :nel]
        if len(shape) == 3:
            v = v.rearrange("p (a b) -> p a b", a=shape[1])
        return v

    def buf(self, name):
        b = Buf(name)
        b.rs = list(self.pending.values())
        self.live.append((self.off, b))
        return b

    def mark(self):
        return self.off

    def release(self, m):
        keep = []
        for o, b in self.live:
            if o >= m:
                evs = list(b.rs)
                if b.w is not None:
                    evs.append(b.w)
                for ev in evs:
                    cur = self.pending.get(ev[0])
                    if cur is None or cur[1] < ev[1]:
                        self.pending[ev[0]] = ev
            else:
                keep.append((o, b))
        self.live = keep
        self.off = m


def build_program(npp, npr, ncst, stage="full"):
    CUT = int(os.environ.get("MK_CUT", "99"))
    nc = bass.Bass("TRN2", target_bir_lowering=False)
    dt_in = lambda name, shape: nc.dram_tensor(name, list(shape), F32, kind="ExternalInput").ap()
    xT_d = dt_in("xT", (D, L))
    pp_d = dt_in("pp", (128, npp))
    pr_d = dt_in("pr", (128, npr))
    cst_d = dt_in("cst", (128, ncst))
    wg_d = [dt_in("wg1", (D, DFF)), dt_in("wg2", (D, DFF))]
    wu_d = [dt_in("wu1", (D, DFF)), dt_in("wu2", (D, DFF))]
    wd_d = [dt_in("wd1", (DFF, D)), dt_in("wd2", (DFF, D))]
    win_d = dt_in("win", (D, D_IN))
    wout_d = dt_in("wout", (D, D))
    w2_d = dt_in("w2", (64, 512))
    a2_d = dt_in("a2", (64, 512))
    g2_d = dt_in("g2", (128, 512))
    out_d = nc.dram_tensor("outT", [D, L], F32, kind="ExternalOutput").ap()
    DBG = bool(int(os.environ.get("MK_DBG", "0")))
    if DBG:
        dbg_d = nc.dram_tensor("dbgy", [D, L], F32, kind="ExternalOutput").ap()
        dbgv = dbg_d.rearrange("(c p) t -> p c t", p=128)
    b_dbg = Buf("dbg")

    S = Sched()
    with ExitStack() as st:
        try:
            st.enter_context(nc.allow_low_precision("bf16 matmul operands by design"))
        except Exception:
            pass
        ARENA_BYTES = 207 * 1024
        arena_t = st.enter_context(nc.sbuf_tensor("arena", [128, ARENA_BYTES // 4], F32))
        A = Arena(arena_t, ARENA_BYTES)
        psum = lambda name, shape, dt=F32: st.enter_context(nc.psum_tensor(name, list(shape), dt))

        def em(eng, fn, r=(), w=(), **kw):
            return S.emit(eng, fn, reads=r, writes=w, **kw)

        def run_streams(gens):
            t_now = max(S.t_eng.values())
            S.head = {k: t_now for k in gens}
            gens = dict(gens)
            while gens:
                sid = min(gens, key=lambda k: S.head[k])
                S.stream = sid
                try:
                    r_ = next(gens[sid])
                    if r_ == "blocked":
                        others = [S.head[k] for k in gens if k != sid]
                        S.head[sid] = (max(others) if others else S.head[sid]) + 1e-3
                except StopIteration:
                    del gens[sid]
            S.stream = None

        def em_pe_rt(fA, fB, r=(), w=()):
            evA = S.emit("pe", fA, reads=r, writes=w, cost=MODEL_RT)
            return S.emit("pe", fB, reads=r, writes=w, force=[evA], cost=MODEL_RT)

        ACT = lambda out, in_, func, **kw: (lambda e: e.activation(out=out, in_=in_, func=func, **kw))
        TT = lambda out, a, b, op: (lambda e: e.tensor_tensor(out=out, in0=a, in1=b, op=op))
        TS = lambda out, a, s1, s2, op0, op1: (lambda e: e.tensor_scalar(out=out, in0=a, scalar1=s1, scalar2=s2, op0=op0, op1=op1))
        STT = lambda out, a, s, b, op0, op1: (lambda e: e.scalar_tensor_tensor(out=out, in0=a, scalar=s, in1=b, op0=op0, op1=op1))
        TCP = lambda out, in_: (lambda e: e.tensor_copy(out=out, in_=in_))

        xT = A.alloc((128, 8, L))
        b_x = [[A.buf("x%d_%d" % (c, g)) for g in range(4)] for c in range(8)]
        pp = A.alloc((128, npp)); b_pp = A.buf("pp")
        pr = A.alloc((128, npr)); b_pr = A.buf("pr")
        NCF = 512
        cstf = A.alloc((128, NCF)); b_cstf = A.buf("cstf")
        cstb = A.alloc((128, ncst), BF16); b_cstb = A.buf("cstb")
        epsc = A.alloc((128, 8)); b_eps = A.buf("eps")
        PS = [psum("ps%d" % i, (128, 512))[:, :] for i in range(7)]
        b_ps = [Buf("ps%d" % i) for i in range(7)]
        PTt = psum("ptt", (128, 1024), BF16)[:, :]
        PT = [PTt[:, i * 512:(i + 1) * 512] for i in range(2)]
        b_pt = [Buf("pt")] * 2

        CL = CONST_LAYOUT
        csf = lambda k, n: cstf[:, CL[k] - CL["rst"]:CL[k] - CL["rst"] + n]
        csb = lambda k, n: cstb[:, CL[k]:CL[k] + n]
        ppc = lambda k, c: pp[:, PP_LAYOUT[k] + c:PP_LAYOUT[k] + c + 1]
        prr = lambda k, n: pr[:, PR_LAYOUT[k]:PR_LAYOUT[k] + n]

        omm = A.alloc((128, 14)); b_omm = A.buf("omm")
        carry = A.alloc((128, 14)); b_carry = [A.buf("carry%d" % c) for c in range(14)]
        halo = A.alloc((128, 8, 4)); b_halo = [A.buf("halo%d" % c) for c in range(8)]
        Tst = [A.alloc((128, 4, 64)) for _ in range(2)]; b_T = [A.buf("T0"), A.buf("T1")]
        Tbf = [A.alloc((128, 4, 64), BF16) for _ in range(2)]; b_Tb = [A.buf("Tb0"), A.buf("Tb1")]
        sst = A.alloc((128, 8, 64)); b_sst = A.buf("sst")
        sstb = A.alloc((128, 512), BF16); b_sstb = A.buf("sstb")
        Arow = A.alloc((128, 8)); b_Arow = A.buf("Arow")
        w2b = A.alloc((128, 512), BF16); b_w2b = A.buf("w2b")
        g2b = A.alloc((128, 512), BF16); b_g2b = A.buf("g2b")

        xv = xT_d.rearrange("(c p) t -> p c t", p=128)
        for g in range(4):
            em("sp", lambda e, g=g: e.dma_start(out=xT[:, :, g * 512:(g + 1) * 512], in_=xv[:, :, g * 512:(g + 1) * 512]),
               w=[b_x[c][g] for c in range(8)], dma="dx%d" % g)
        em("sp", lambda e: e.dma_start(out=pp, in_=pp_d), w=[b_pp], dma="dc0")
        em("sp", lambda e: e.dma_start(out=pr, in_=pr_d), w=[b_pr], dma="dc1")
        em("sp", lambda e: e.dma_start(out=cstf, in_=cst_d[:, CONST_LAYOUT["rst"]:CONST_LAYOUT["rst"] + NCF]), w=[b_cstf], dma="dc2")
        em("pool", lambda e: e.dma_start(out=cstb, in_=cst_d), w=[b_cstb], dma="dc3")
        em("pool", lambda e: [e.dma_start(out=w2b[0:64, :], in_=w2_d), e.dma_start(out=w2b[64:128, :], in_=a2_d)],
           w=[b_w2b], dma="dc4", ninc=2)
        em("pool", lambda e: e.dma_start(out=g2b, in_=g2_d), w=[b_g2b], dma="dc5")

        def f_eps(e):
            e.memset(epsc[:, 0:1], 1e-6)
            e.memset(epsc[:, 1:2], 64e-5)
            e.memset(epsc[:, 2:3], 1e-5)
            e.memset(epsc[:, 3:4], 1.0)
            e.memset(epsc[:, 4:5], 1e-24)
            e.memset(carry, 0.0)
            e.memset(halo, 0.0)
            e.memset(Tst[0], 0.0)
            e.memset(Tbf[0], 0.0)
            e.memset(sstb, 0.0)
            return e.memset(sst, 0.0)
        em("dve", f_eps, w=[b_eps, b_T[0], b_Tb[0], b_sst, b_sstb] + b_carry + b_halo)
        em("dve", TS(omm, pp[:, PP_LAYOUT["mu"]:PP_LAYOUT["mu"] + 14], -1.0, 1.0, ALU.mult, ALU.add), r=[b_pp], w=[b_omm], small=True)

        def f_arow(e):
            return e.activation(out=Arow, in_=prr("alog", 8), func=AF.Exp)
        em("act", f_arow, r=[b_pr], w=[b_Arow], small=True)
        em("dve", TS(Arow, Arow, -1.0, 0.0, ALU.mult, ALU.add), r=[b_Arow], w=[b_Arow], small=True)

        sq_holder = [None, None]
        rstd = [A.alloc((128, 512))] * 2; b_rstd = [A.buf("rstd0")] * 2
        norm_ctr = [0]

        def rms_group(g, gkey, out_fn, out_bufs, psi):
            sq, bsq_l = sq_holder
            tok = slice(g * 512, (g + 1) * 512)
            k = norm_ctr[0] % 2
            norm_ctr[0] += 1
            em("act", ACT(sq, xT[:, :, tok], AF.Square), r=[b_x[c][g] for c in range(8)], w=bsq_l)

            def f_mm(e):
                for c in range(8):
                    ins = e.matmul(PS[psi], lhsT=csb("ones", 128), rhs=sq[:, c, :], start=(c == 0), stop=(c == 7))
                return ins
            em("pe", f_mm, r=list(bsq_l) + [b_cstb], w=[b_ps[psi]])

            def f_rs(e):
                e.activation(out=rstd[k], in_=PS[psi], func=AF.Ln, bias=epsc[:, 0:1], scale=1.0 / D)
                return e.activation(out=rstd[k], in_=rstd[k], func=AF.Exp, scale=-0.5)
            em("act", f_rs, r=[b_ps[psi], b_eps], w=[b_rstd[k]])
            for c in range(8):
                em("dve", STT(out_fn(c), xT[:, c, tok], ppc(gkey, c), rstd[k], ALU.mult, ALU.mult),
                   r=[b_x[c][g], b_pp, b_rstd[k]], w=[out_bufs[c]])

        def ffn(fi, gkey):
            m0 = A.mark()
            sq_holder[0] = A.alloc((128, 8, 512), BF16); sq_holder[1] = [A.buf("sq")]
            hT = A.alloc((128, 8, 1024), BF16)
            b_h = [[A.buf("h%d_%d" % (c, g)) for g in range(2)] for c in range(8)]
            actT = A.alloc((128, NFC, 1024), BF16)
            b_act = [[A.buf("a%d_%d" % (f, g)) for g in range(2)] for f in range(NFC)]
            NWB = 2
            wgb = [A.alloc((128, 8, 256), BF16) for i in range(NWB)]; b_wg = [A.buf("wg%d" % i) for i in range(NWB)]
            wub = [A.alloc((128, 8, 256), BF16) for i in range(NWB)]; b_wu = [A.buf("wu%d" % i) for i in range(NWB)]
            wdb = [A.alloc((128, NFC, 256), BF16) for i in range(2)]; b_wd = [A.buf("wd%d" % i) for i in range(2)]
            sgt = [A.alloc((128, 512)) for i in range(2)]; b_sg = [A.buf("sg0"), A.buf("sg1")]
            wgv = wg_d[fi].rearrange("(k p) c -> p k c", p=128)
            wuv = wu_d[fi].rearrange("(k p) c -> p k c", p=128)
            wdv = wd_d[fi].rearrange("(f p) c -> p f c", p=128)
            wq = 0
            dq_ctr = 0
            for hf in range(2):
                for g2 in range(2):
                    g = hf * 2 + g2
                    rms_group(g, gkey, lambda c, g2=g2: hT[:, c, g2 * 512:(g2 + 1) * 512], [b_h[c][g2] for c in range(8)], 6)
                for jb in range(DFF // 256):
                    wi = wq % NWB
                    wq += 1
                    cs_ = slice(jb * 256, (jb + 1) * 256)
                    em("pool", lambda e, wi=wi, cs_=cs_: e.dma_start(out=wgb[wi], in_=wgv[:, :, cs_]),
                       w=[b_wg[wi]], dma="dwg%d" % wi)
                    em("pool", lambda e, wi=wi, cs_=cs_: e.dma_start(out=wub[wi], in_=wuv[:, :, cs_]),
                       w=[b_wu[wi]], dma="dwu%d" % wi)
                    for sub in range(2):
                        fc = jb * 2 + sub
                        for g2 in range(2):
                            it = (fc * 2 + g2)
                            pg, pu = (it % 2) * 2, (it % 2) * 2 + 1
                            tk = slice(g2 * 512, (g2 + 1) * 512)
                            wc = slice(sub * 128, (sub + 1) * 128)

                            def f_g(e, wi=wi, wc=wc, tk=tk, pg=pg):
                                for k in range(8):
                                    ins = e.matmul(PS[pg], lhsT=wgb[wi][:, k, wc], rhs=hT[:, k, tk], start=(k == 0), stop=(k == 7))
                                return ins
                            em("pe", f_g, r=[b_wg[wi]] + [b_h[c][g2] for c in range(8)], w=[b_ps[pg]])

                            def f_u(e, wi=wi, wc=wc, tk=tk, pu=pu):
                                for k in range(8):
                                    ins = e.matmul(PS[pu], lhsT=wub[wi][:, k, wc], rhs=hT[:, k, tk], start=(k == 0), stop=(k == 7))
                                return ins
                            em("pe", f_u, r=[b_wu[wi]] + [b_h[c][g2] for c in range(8)], w=[b_ps[pu]])
                            si = it % 2
                            em("act", ACT(sgt[si], PS[pg], AF.Silu), r=[b_ps[pg]], w=[b_sg[si]])
                            em("dve", TT(actT[:, fc, tk], PS[pu], sgt[si], ALU.mult),
                               r=[b_ps[pu], b_sg[si]], w=[b_act[fc][g2]])
                for dq in range(4):
                    wi = dq_ctr % 2
                    dq_ctr += 1
                    cs_ = slice(dq * 256, (dq + 1) * 256)
                    em("pool", lambda e, wi=wi, cs_=cs_: [
                        e.dma_start(out=wdb[wi][:, 0:11, :], in_=wdv[:, 0:11, cs_]),
                        e.dma_start(out=wdb[wi][:, 11:22, :], in_=wdv[:, 11:22, cs_])],
                        w=[b_wd[wi]], dma="dwd%d" % wi, ninc=2)
                    for sub in range(2):
                        dc = dq * 2 + sub
                        for g2 in range(2):
                            g = hf * 2 + g2
                            pi = 4 + (dc * 2 + g2) % 2
                            tk = slice(g2 * 512, (g2 + 1) * 512)
                            wc = slice(sub * 128, (sub + 1) * 128)

                            def f_d(e, wi=wi, wc=wc, tk=tk, pi=pi):
                                for f in range(NFC):
                                    ins = e.matmul(PS[pi], lhsT=wdb[wi][:, f, wc], rhs=actT[:, f, tk], start=(f == 0), stop=(f == NFC - 1))
                                return ins
                            em("pe", f_d, r=[b_wd[wi]] + [b_act[f][g2] for f in range(NFC)], w=[b_ps[pi]])
                            xs_ = xT[:, dc, g * 512:(g + 1) * 512]
                            em("dve", STT(xs_, PS[pi], 0.5, xs_, ALU.mult, ALU.add),
                               r=[b_ps[pi], b_x[dc][g]], w=[b_x[dc][g]])
            A.release(m0)

        def mixer():
            m0 = A.mark()
            hTb = A.alloc((128, 8, 512), BF16); b_hb = [A.buf("hb%d" % c) for c in range(8)]
            NW = 2
            winb = [A.alloc((128, 8, 256), BF16) for i in range(NW)]; b_win = [A.buf("win%d" % i) for i in range(NW)]
            ycat = A.alloc((128, 8, 512), BF16); b_yc = [A.buf("yc%d" % c) for c in range(8)]
            sq_holder[0] = ycat; sq_holder[1] = b_yc
            winv = win_d.rearrange("(k p) c -> p k c", p=128)
            woutv = wout_d.rearrange("(k p) c -> p k c", p=128)
            wctr = [0]

            def load_w(view, c0, ncol):
                wi = wctr[0] % NW
                wctr[0] += 1
                em("pool", lambda e, wi=wi: e.dma_start(out=winb[wi][:, :, 0:ncol], in_=view[:, :, c0:c0 + ncol]),
                   w=[b_win[wi]], dma="dwin%d" % wi)
                return wi

            def proj_fm(wi, sub, psi):
                wc = slice(sub * 128, (sub + 1) * 128)

                def f(e):
                    for k in range(8):
                        ins = e.matmul(PS[psi], lhsT=winb[wi][:, k, wc], rhs=hTb[:, k, :], start=(k == 0), stop=(k == 7))
                    return ins
                em("pe", f, r=[b_win[wi]] + b_hb, w=[b_ps[psi]], cost=2.5)

            for blk in range(4):
                rms_group(blk, "gm", lambda c: hTb[:, c, :], b_hb, 6)
                mR = A.mark()
                qv3 = A.alloc((128, 4, 512)); b_qv3 = [A.buf("qv%d" % c) for c in range(4)]
                tmu = [A.alloc((128, 512)) for _ in range(2)]; b_tmu = [A.buf("tmu0"), A.buf("tmu1")]
                sgd = A.alloc((128, 512), BF16); b_sgd = A.buf("sgd")
                AT = [A.alloc((128, 512), BF16) for _ in range(4)]; b_AT = [A.buf("AT%d" % i) for i in range(4)]
                RT = [A.alloc((128, 512), BF16) for _ in range(4)]; b_RT = [A.buf("RT%d" % i) for i in range(4)]
                BT = [A.alloc((128, 512), BF16) for _ in range(4)]; b_BT = [A.buf("BT%d" % i) for i in range(4)]
                KT = [A.alloc((128, 512), BF16) for _ in range(4)]; b_KT = [A.buf("KT%d" % i) for i in range(4)]
                VB = [A.alloc((128, 512), BF16) for _ in range(4)]; b_VB = [A.buf("VB%d" % i) for i in range(4)]
                rkb = [A.alloc((128, 512), BF16) for _ in range(4)]; b_rkb = [A.buf("rkb%d" % i) for i in range(4)]
                dWt = A.alloc((128, 4, 8)); b_dW = [A.buf("dW%d" % i) for i in range(4)]
                Ktok = [A.alloc((128, 512), BF16) for _ in range(1)]; b_Ktok = [A.buf("Ktok%d" % i) for i in range(2)]
                Btok = [A.alloc((128, 512), BF16) for _ in range(1)]; b_Btok = [A.buf("Btok%d" % i) for i in range(2)]
                Vtok = [A.alloc((128, 512), BF16) for _ in range(1)]; b_Vtok = [A.buf("Vtok%d" % i) for i in range(2)]
                Utok = [A.alloc((128, 512), BF16) for _ in range(1)]; b_Utok = [[A.buf("Utok%d_%d" % (i, e_)) for e_ in range(2)] for i in range(2)]
                MT = [A.alloc((128, 512), BF16) for _ in range(1)]; b_MT = [A.buf("MT%d" % i) for i in range(2)]
                AakT = [A.alloc((128, 512), BF16) for _ in range(1)]; b_Aak = [A.buf("Aak%d" % i) for i in range(2)]
                ArbT = [A.alloc((128, 512), BF16) for _ in range(1)]; b_Arb = [A.buf("Arb%d" % i) for i in range(2)]
                ArkT = [A.alloc((128, 512), BF16) for _ in range(1)]; b_Ark = [A.buf("Ark%d" % i) for i in range(2)]
                Qa = [A.alloc((128, 512), BF16) for _ in range(2)]; b_Qa = [A.buf("Qa0"), A.buf("Qa1")]
                Qt = [A.alloc((128, 512), BF16) for _ in range(2)]; b_Qt = [A.buf("Qt0"), A.buf("Qt1")]
                Pm = [A.alloc((128, 512), BF16) for _ in range(2)]; b_Pm = [A.buf("Pm0"), A.buf("Pm1")]
                Zb = A.alloc((128, 512), BF16); b_Zb = [A.buf("Zb0"), A.buf("Zb1")]
                ysb = A.alloc((128, 4, 512)); b_ysb = A.buf("ysb")
                Ttmp = tmu[0][:, 0:256].rearrange("p (a b) -> p a b", a=4); b_Ttmp = b_tmu[0]
                mP = A.mark()
                qrk = A.alloc((128, 8, 512)); b_qrk = [A.buf("q%d" % c) for c in range(8)]
                tw = A.alloc((128, 512), BF16); b_tw = A.buf("tw")
                tt_ = [A.alloc((128, 512)) for _ in range(4)]; b_t = [A.buf("t%d" % i) for i in range(4)]
                s0 = A.alloc((128, 512), BF16); b_s0 = A.buf("s0")
                s1 = A.alloc((128, 512), BF16); b_s1 = A.buf("s1")
                b_q = b_qrk + b_qv3 + [b_t[2], b_t[3]]
                qv = lambda c, qrk=qrk, tt_=tt_, qv3=qv3: (qrk[:, c, :] if c < 8 else (qv3[:, c - 8, :] if c < 12 else tt_[c - 10]))

                done_chunks = set()

                def gen_inproj():
                    order = [6, 0, 2, 4, 1, 3, 5]
                    pctr = 0
                    for jb in order:
                        wi = load_w(winv, jb * 256, 256)
                        for sub in range(2):
                            c = jb * 2 + sub
                            psi = pctr % 2
                            tm = pctr % 2
                            pctr += 1
                            proj_fm(wi, sub, psi)
                            yield
                            em("act", ACT(tmu[tm], PS[psi], AF.Copy, scale=ppc("mu", c)), r=[b_ps[psi], b_pp], w=[b_tmu[tm]])
                            yield
                            em("dve", STT(qv(c)[:, 1:512], PS[psi][:, 1:512], omm[:, c:c + 1], tmu[tm][:, 0:511], ALU.mult, ALU.add),
                               r=[b_ps[psi], b_omm, b_tmu[tm]], w=[b_q[c]])
                            yield
                            em("dve", STT(qv(c)[:, 0:1], PS[psi][:, 0:1], omm[:, c:c + 1], carry[:, c:c + 1], ALU.mult, ALU.add),
                               r=[b_ps[psi], b_omm, b_carry[c]], w=[b_q[c]], small=True)
                            yield
                            em("act", ACT(carry[:, c:c + 1], tmu[tm][:, 511:512], AF.Copy), r=[b_tmu[tm]], w=[b_carry[c]], small=True)
                            yield
                            done_chunks.add(c)
                        if jb == 6:
                            def f_tw(e):
                                e.activation(out=tw[0:64, :], in_=tt_[2][0:64, :], func=AF.Tanh)
                                return e.activation(out=tw[64:128, :], in_=tt_[2][64:128, :], func=AF.Copy)
                            em("act", f_tw, r=[b_q[12]], w=[b_tw])
                            yield
                            em("act", ACT(sgd, tt_[3], AF.Sigmoid), r=[b_q[13]], w=[b_sgd])
                            yield
                            done_chunks.add("tw")


                def gen_prep(ccs, T4, bT4, s0_, bs0_, pw, pa, pb):
                    for cc in ccs:
                        while not {"tw", cc, 4 + cc, 8 + cc} <= done_chunks:
                            yield "blocked"
                        wcs = slice(cc * 128, (cc + 1) * 128)
                        r_ = qv(cc)
                        k_ = qv(4 + cc)
                        t0, t1, t2, t3 = T4
                        bt0, bt1, bt2, bt3 = bT4
                        em("pe", lambda e, wcs=wcs, pw=pw: e.matmul(PS[pw], lhsT=w2b[0:64, wcs], rhs=tw[0:64, :], start=True, stop=True, tile_position=(0, 0)),
                           r=[b_w2b, b_tw], w=[b_ps[pw]])
                        yield
                        em("pe", lambda e, wcs=wcs, pa=pa: e.matmul(PS[pa], lhsT=w2b[64:128, wcs], rhs=tw[64:128, :], start=True, stop=True, tile_position=(64, 0)),
                           r=[b_w2b, b_tw], w=[b_ps[pa]])
                        yield
                        em("act", ACT(t0, PS[pw], AF.Sigmoid, bias=ppc("w0", cc)), r=[b_ps[pw], b_pp], w=[bt0])
                        yield
                        em("dve", lambda e, t0=t0, t1=t1: e.tensor_tensor_scan(out=t1, data0=csf("rst", 512), data1=t0, initial=0.0, op0=ALU.mult, op1=ALU.add),
                           r=[bt0, b_cstf], w=[bt1])
                        yield
                        em("dve", TT(t2, t1, t0, ALU.subtract), r=[bt1, bt0], w=[bt2])
                        yield
                        em("act", ACT(t2, t2, AF.Exp, scale=-DECAY_C), r=[bt2], w=[bt2])
                        yield
                        em("act", ACT(t3, t1, AF.Exp, scale=-DECAY_C), r=[bt1], w=[bt3])
                        yield
                        em("act", ACT(t1, t1, AF.Exp, scale=DECAY_C), r=[bt1], w=[bt1])
                        yield
                        em("act", ACT(dWt[:, cc, :], t3.rearrange("p (c j) -> p c j", j=64)[:, :, 63], AF.Copy), r=[bt3], w=[b_dW[cc]], small=True)
                        yield
                        em("dve", TT(RT[cc], r_, t3, ALU.mult), r=[b_q[cc], bt3], w=[b_RT[cc]])
                        yield
                        em("dve", TS(t3, k_, ppc("kk", cc), 0.0, ALU.mult, ALU.add), r=[b_q[4 + cc], b_pp, b_RT[cc]], w=[bt3])
                        yield
                        em("act", ACT(s0_, t3, AF.Square), r=[bt3], w=[bs0_])
                        yield
                        em("pe", lambda e, s0_=s0_, pb=pb: e.matmul(PS[pb], lhsT=csb("blk", 128), rhs=s0_, start=True, stop=True), r=[b_cstb, bs0_], w=[b_ps[pb]])
                        yield

                        def f_rn(e, t0=t0, pb=pb):
                            e.activation(out=t0, in_=PS[pb], func=AF.Ln, bias=epsc[:, 4:5])
                            return e.activation(out=t0, in_=t0, func=AF.Exp, scale=-0.5)
                        em("act", f_rn, r=[b_ps[pb], b_eps, bt2], w=[bt0])
                        yield
                        em("dve", TT(t3, t3, t0, ALU.mult), r=[bt3, bt0], w=[bt3])
                        yield
                        em("dve", STT(AT[cc], t3, -1.0, t2, ALU.mult, ALU.mult), r=[bt3, bt2], w=[b_AT[cc]])
                        yield
                        em("act", ACT(t0, PS[pa], AF.Sigmoid, bias=ppc("a0", cc)), r=[b_ps[pa], b_pp, bt3], w=[bt0])
                        yield
                        em("dve", TT(t2, t3, t0, ALU.mult), r=[bt3, bt0, b_AT[cc]], w=[bt2])
                        yield
                        em("dve", TT(BT[cc], t2, t1, ALU.mult), r=[bt2, bt1], w=[b_BT[cc]])
                        yield
                        em("dve", TS(t2, t0, -1.0, ppc("ka", cc), ALU.add, ALU.mult), r=[bt0, b_pp, b_BT[cc]], w=[bt2])
                        yield
                        em("dve", STT(t2, t2, 1.0, k_, ALU.add, ALU.mult), r=[bt2, b_q[4 + cc]], w=[bt2])
                        yield
                        em("dve", TT(KT[cc], t2, t1, ALU.mult), r=[bt2, bt1], w=[b_KT[cc]])
                        yield
                        em("dve", STT(rkb[cc], r_, ppc("rk", cc), t2, ALU.mult, ALU.mult), r=[b_q[cc], b_pp, bt2], w=[b_rkb[cc]])
                        yield
                        em("act", ACT(VB[cc], qv(8 + cc), AF.Copy), r=[b_q[8 + cc]], w=[b_VB[cc]])
                        yield


                tB = [ysb[:, i_, :] for i_ in range(4)]; b_tB = [A.buf("tB%d" % i_) for i_ in range(4)]
                s0B = A.alloc((128, 512), BF16); b_s0B = A.buf("s0B")
                run_streams({"ip": gen_inproj(), "pA": gen_prep([0, 2], tt_, b_t, s0, b_s0, 4, 5, 6), "pB": gen_prep([1, 3], tB, b_tB, s0B, b_s0B, 2, 3, 2)})
                A.release(mP)

                pSb = [A.alloc((128, 516)) for _ in range(2)]; b_pSb = [A.buf("pSb0"), A.buf("pSb1")]
                xc = A.alloc((128, 512)); b_xc = A.buf("xc")
                xsT = A.alloc((128, 4, 512), BF16); b_xsT = [A.buf("xsT%d" % c) for c in range(4)]
                BCT = A.alloc((128, 4, 512), BF16); b_BCT = [A.buf("BCT%d" % c) for c in range(4)]
                zs = A.alloc((128, 4, 512), BF16); b_zs = [A.buf("zs%d" % c) for c in range(4)]
                dtk = A.alloc((128, 4, 8)); b_dtk = A.buf("dtk")
                atk = A.alloc((128, 4, 8)); b_atk = A.buf("atk")
                xtok = A.alloc((128, 8, 64), BF16); b_xtok = A.buf("xtok")
                Bk = A.alloc((128, 256), BF16); b_Bk = A.buf("Bk")
                xdt = A.alloc((128, 8, 64), BF16); b_xdt = A.buf("xdt")
                xdd = A.alloc((128, 8, 64), BF16); b_xdd = A.buf("xdd")
                acs = A.alloc((128, 32)); b_acs = A.buf("acs")
                E32 = A.alloc((128, 32)); b_E32 = A.buf("E32")
                dte = A.alloc((128, 8)); b_dte = A.buf("dte")
                rarh = A.alloc((128, 8, 64), BF16); rarl = A.alloc((128, 8, 64), BF16); b_rar = A.buf("rar")
                ahi = A.alloc((128, 4, 8), BF16); alo = A.alloc((128, 4, 8), BF16); atmp = A.alloc((128, 4, 8))
                b_ahi = A.buf("ahi"); b_alo = A.buf("alo"); b_atmp = A.buf("atmp")
                seg = A.alloc((128, 8, 64)); b_seg = A.buf("seg")
                scm = A.alloc((128, 2, 64)); b_scm = A.buf("scm")
                Wt = A.alloc((128, 8, 64), BF16); b_Wt = A.buf("Wt")
                yt = A.alloc((128, 8, 64)); b_yt = A.buf("yt")
                yt2 = A.alloc((128, 8, 64)); b_yt2 = A.buf("yt2")
                stmp = yt2; b_stmp = b_yt2
                ssq = A.alloc((128, 2)); b_ssq = A.buf("ssq")
                ytb = A.alloc((128, 512), BF16); b_ytb = A.buf("ytb")

                def gen_rwkv():
                    v4 = lambda ap: ap.rearrange("p (m t) -> p m t", t=128)
                    for cp in range(4):
                        pb = 0
                        tsl = slice(cp * 128, (cp + 1) * 128)
                        for (src, bsrc, dst, bdst, pti) in ((KT, b_KT, Ktok, b_Ktok, 0), (BT, b_BT, Btok, b_Btok, 1), (VB, b_VB, Vtok, b_Vtok, 0)):
                            def f_tr(e, src=src, pti=pti, tsl=tsl):
                                for cc in range(4):
                                    ins = e.transpose(out=PT[pti][:, cc * 128:(cc + 1) * 128], in_=src[cc][:, tsl], identity=csb("ident", 128))
                                return ins
                            em("pe", f_tr, r=list(bsrc) + [b_cstb], w=[b_pt[pti]])
                            yield
                            em("act", ACT(dst[pb], PT[pti], AF.Copy), r=[b_pt[pti]], w=[bdst[pb]])
                            yield

                        yield

                        def amat(px, lt, rt, blt, brt, dstt, bdst, mask):
                            def mk(par):
                                def f(e, cp=cp):
                                    for h in range(par, 8, 2):
                                        cc, po = h // 2, 64 * par
                                        for e_ in range(2):
                                            cs_ = slice((2 * cp + e_) * 64, (2 * cp + e_ + 1) * 64)
                                            ins = e.matmul(PS[px][e_ * 64:(e_ + 1) * 64, h * 64:(h + 1) * 64],
                                                           lhsT=lt[cc][po:po + 64, cs_], rhs=rt[cc][po:po + 64, cs_],
                                                           start=True, stop=True, tile_position=(po, e_ * 64))
                                    return ins
                                return f
                            em_pe_rt(mk(0), mk(1), r=list(blt) + list(brt), w=[b_ps[px]])
                            em("dve", TT(dstt, PS[px], csb(mask, 512), ALU.mult), r=[b_ps[px], b_cstb], w=[bdst])
                        amat(0, BT, AT, b_BT, b_AT, Qa[0], b_Qa[0], "su")
                        yield
                        yield
                        amat(1, AT, BT, b_AT, b_BT, Qt[0], b_Qt[0], "sl")
                        yield
                        yield
                        amat(2, KT, AT, b_KT, b_AT, AakT[pb], b_Aak[pb], "su")
                        yield
                        yield
                        amat(3, BT, RT, b_BT, b_RT, ArbT[pb], b_Arb[pb], "iu")
                        yield
                        yield
                        amat(4, KT, RT, b_KT, b_RT, ArkT[pb], b_Ark[pb], "iu")
                        yield
                        yield
                        em("dve", TT(Pm[0], Qa[0], csb("eye", 512), ALU.add), r=[b_Qa[0], b_cstb], w=[b_Pm[0]])
                        yield

                        def sqmat(px, lt, rt, blt, brt):
                            def mk(e_):
                                def f(e):
                                    es = slice(e_ * 64, (e_ + 1) * 64)
                                    for h in range(8):
                                        hs = slice(h * 64, (h + 1) * 64)
                                        ins = e.matmul(PS[px][es, hs], lhsT=lt[es, hs], rhs=rt[es, hs], start=True, stop=True,
                                                       tile_position=(e_ * 64, e_ * 64))
                                    return ins
                                return f
                            em_pe_rt(mk(0), mk(1), r=[blt, brt], w=[b_ps[px]])
                        cur = 0
                        for lv in range(5):
                            nxt = 1 - cur
                            sqmat(2, Qa[cur], Qt[cur], b_Qa[cur], b_Qt[cur])
                            yield
                            if lv < 4:
                                sqmat(3, Qt[cur], Qa[cur], b_Qt[cur], b_Qa[cur])
                                yield
                            em("act", ACT(Qt[nxt], PS[2], AF.Copy), r=[b_ps[2]], w=[b_Qt[nxt]])
                            yield
                            if lv < 4:
                                em("dve", TCP(Qa[nxt], PS[3]), r=[b_ps[3]], w=[b_Qa[nxt]])
                                yield
                            sqmat(lv % 2, Qt[nxt], Pm[cur], b_Qt[nxt], b_Pm[cur])
                            yield
                            dstP = MT[pb] if lv == 4 else Pm[nxt]
                            bdstP = b_MT[pb] if lv == 4 else b_Pm[nxt]
                            em("dve", TT(dstP, PS[lv % 2], Pm[cur], ALU.add), r=[b_ps[lv % 2], b_Pm[cur]], w=[bdstP])
                            yield
                            cur = nxt
                            yield

                        for e_ in range(2):
                            c = 2 * cp + e_
                            gidx = blk * 8 + c
                            tc_, tn_ = gidx % 2, (gidx + 1) % 2
                            es = slice(e_ * 64, (e_ + 1) * 64)
                            cs_ = slice(c * 64, (c + 1) * 64)

                            def mk_z(first, es=es, cs_=cs_, tc_=tc_, e_=e_):
                                def f(e):
                                    ins = None
                                    if first:
                                        st_ = True
                                        for h in range(8):
                                            cc, par = h // 2, h % 2
                                            hs = slice(h * 64, (h + 1) * 64)
                                            if par == e_:
                                                e.matmul(PS[1][es, hs], lhsT=AT[cc][e_ * 64:e_ * 64 + 64, cs_], rhs=Tbf[tc_][e_ * 64:e_ * 64 + 64, cc, :],
                                                         start=st_, stop=False, tile_position=(e_ * 64, e_ * 64))
                                                st_ = False
                                            ins = e.matmul(PS[1][es, hs], lhsT=AakT[pb][es, hs], rhs=Vtok[pb][es, hs],
                                                           start=st_, stop=False, tile_position=(e_ * 64, e_ * 64))
                                            st_ = False
                                    else:
                                        par = 1 - e_
                                        po = 64 * par
                                        for h in range(par, 8, 2):
                                            cc = h // 2
                                            hs = slice(h * 64, (h + 1) * 64)
                                            ins = e.matmul(PS[1][es, hs], lhsT=AT[cc][po:po + 64, cs_], rhs=Tbf[tc_][po:po + 64, cc, :],
                                                           start=False, stop=(h >= 6), tile_position=(po, e_ * 64))
                                    return ins
                                return f
                            em_pe_rt(mk_z(True), mk_z(False), r=list(b_AT) + [b_Tb[tc_], b_Aak[pb], b_Vtok[pb]], w=[b_ps[1]])
                            yield
                            em("act", ACT(Zb[es, :], PS[1][es, :], AF.Copy), r=[b_ps[1]], w=[b_Zb[e_]])
                            yield
                            yield

                            def f_u(e, es=es, e_=e_):
                                for h in range(8):
                                    hs = slice(h * 64, (h + 1) * 64)
                                    ins = e.matmul(PS[2][es, hs], lhsT=MT[pb][es, hs], rhs=Zb[es, hs], start=True, stop=True,
                                                   tile_position=(e_ * 64, e_ * 64))
                                return ins
                            em("pe", f_u, r=[b_MT[pb], b_Zb[e_]], w=[b_ps[2]])
                            yield
                            em("dve", TCP(Utok[pb][es, :], PS[2][es, :]), r=[b_ps[2]], w=[b_Utok[pb][e_]])
                            yield
                            yield

                            yb = 4 if e_ == 0 else 0

                            def mk_y(first, es=es, cs_=cs_, tc_=tc_, e_=e_, yb=yb):
                                def f(e):
                                    ins = None
                                    if first:
                                        started = [False, False]
                                        for h in range(8):
                                            cc, par = h // 2, h % 2
                                            po = 64 * par
                                            hs = slice(h * 64, (h + 1) * 64)
                                            o_ = PS[yb][po:po + 64, cc * 64:(cc + 1) * 64]
                                            same = (par == e_)
                                            if same:
                                                e.matmul(o_, lhsT=Tbf[tc_][po:po + 64, cc, :], rhs=RT[cc][po:po + 64, cs_],
                                                         start=(not started[par]), stop=False, tile_position=(po, po))
                                                started[par] = True
                                            e.matmul(o_, lhsT=Utok[pb][es, hs], rhs=ArbT[pb][es, hs], start=(not started[par]), stop=False, tile_position=(e_ * 64, po))
                                            started[par] = True
                                            ins = e.matmul(o_, lhsT=Vtok[pb][es, hs], rhs=ArkT[pb][es, hs], start=False, stop=(same and h >= 6), tile_position=(e_ * 64, po))
                                    else:
                                        par = 1 - e_
                                        po = 64 * par
                                        for h in range(par, 8, 2):
                                            cc = h // 2
                                            o_ = PS[yb][po:po + 64, cc * 64:(cc + 1) * 64]
                                            ins = e.matmul(o_, lhsT=Tbf[tc_][po:po + 64, cc, :], rhs=RT[cc][po:po + 64, cs_],
                                                           start=False, stop=(h >= 6), tile_position=(po, po))
                                    return ins
                                return f
                            em_pe_rt(mk_y(True), mk_y(False), r=[b_Tb[tc_], b_Utok[pb][e_], b_Arb[pb], b_Ark[pb], b_Vtok[pb]] + list(b_RT), w=[b_ps[yb]])
                            yield
                            em("act", ACT(ysb[:, :, cs_], PS[yb][:, 0:256].rearrange("p (a b) -> p a b", a=4), AF.Copy), r=[b_ps[yb]], w=[b_ysb] + b_tB)
                            yield

                            def f_t(e, es=es, e_=e_):
                                for h in range(8):
                                    cc, po = h // 2, 64 * (h % 2)
                                    hs = slice(h * 64, (h + 1) * 64)
                                    o_ = PS[3][po:po + 64, cc * 64:(cc + 1) * 64]
                                    e.matmul(o_, lhsT=Btok[pb][es, hs], rhs=Utok[pb][es, hs], start=True, stop=False,
                                             tile_position=(e_ * 64, po))
                                    ins = e.matmul(o_, lhsT=Ktok[pb][es, hs], rhs=Vtok[pb][es, hs], start=False, stop=True,
                                                   tile_position=(e_ * 64, po))
                                return ins
                            em("pe", f_t, r=[b_Btok[pb], b_Utok[pb][e_], b_Ktok[pb], b_Vtok[pb]], w=[b_ps[3]])
                            yield
                            em("dve", TT(Ttmp, PS[3][:, 0:256].rearrange("p (a b) -> p a b", a=4), Tst[tc_], ALU.add),
                               r=[b_ps[3], b_T[tc_]], w=[b_Ttmp], small=True)
                            yield
                            em("dve", TT(Tst[tn_], Ttmp, dWt[:, :, c:c + 1].to_broadcast([128, 4, 64]), ALU.mult),
                               r=[b_Ttmp] + b_dW, w=[b_T[tn_]], small=True)
                            yield
                            em("act", ACT(Tbf[tn_], Tst[tn_], AF.Copy), r=[b_T[tn_]], w=[b_Tb[tn_]])
                            yield
                            yield


                def gen_ssd():
                    wz = [load_w(winv, 1792, 256), load_w(winv, 2048, 256)]
                    for tt4 in range(4):
                        tsl = slice(tt4 * 128, (tt4 + 1) * 128)
                        psi = 5 + tt4 % 2

                        def f_zp(e, tsl=tsl, psi=psi, wz=wz):
                            for half in range(2):
                                for k in range(8):
                                    ins = e.matmul(PS[psi][:, half * 256:(half + 1) * 256], lhsT=hTb[:, k, tsl], rhs=winb[wz[half]][:, k, :],
                                                   start=(k == 0), stop=(k == 7))
                            return ins
                        em("pe", f_zp, r=[b_win[wz[0]], b_win[wz[1]]] + b_hb, w=[b_ps[psi]], cost=2.5)
                        yield
                        em("act", ACT(zs[:, tt4, :], PS[psi], AF.Silu), r=[b_ps[psi]], w=[b_zs[tt4]])
                        yield
                        yield
                    wdt = load_w(winv, 3328, 8)

                    def f_dtp(e, wdt=wdt):
                        for tt4 in range(4):
                            for k in range(8):
                                ins = e.matmul(PS[6][:, tt4 * 8:(tt4 + 1) * 8], lhsT=hTb[:, k, tt4 * 128:(tt4 + 1) * 128], rhs=winb[wdt][:, k, 0:8],
                                               start=(k == 0), stop=(k == 7))
                        return ins
                    em("pe", f_dtp, r=[b_win[wdt]] + b_hb, w=[b_ps[6]])
                    yield
                    em("dve", TT(dtk, PS[6][:, 0:32].rearrange("p (a b) -> p a b", a=4), prr("dtb", 8).unsqueeze(1).to_broadcast([128, 4, 8]), ALU.add),
                       r=[b_ps[6], b_pr], w=[b_dtk], small=True)
                    yield

                    em("act", ACT(dtk, dtk, AF.Exp), r=[b_dtk], w=[b_dtk], small=True)
                    yield
                    em("act", ACT(dtk, dtk, AF.Ln, bias=epsc[:, 3:4]), r=[b_dtk, b_eps], w=[b_dtk], small=True)
                    yield
                    em("dve", TT(atk, dtk, Arow.unsqueeze(1).to_broadcast([128, 4, 8]), ALU.mult), r=[b_dtk, b_Arow], w=[b_atk], small=True)
                    yield
                    em("dve", TCP(ahi, atk), r=[b_atk], w=[b_ahi], small=True)
                    yield
                    em("dve", TT(atmp, atk, ahi, ALU.subtract), r=[b_atk, b_ahi], w=[b_atmp], small=True)
                    yield
                    em("dve", TCP(alo, atmp), r=[b_atmp], w=[b_alo], small=True)
                    yield
                    pctr = 0
                    for jb in range(4):
                        wi = load_w(winv, 2304 + jb * 256, 256)
                        for sub in range(2):
                            c8 = jb * 2 + sub
                            psi = 5 + pctr % 2
                            pctr += 1
                            proj_fm(wi, sub, psi)
                            yield
                            pk = c8 % 2
                            em("act", ACT(pSb[pk][:, 0:3], halo[:, c8, 0:3], AF.Copy), r=[b_halo[c8]], w=[b_pSb[pk]], small=True)
                            yield
                            em("act", ACT(pSb[pk][:, 3:515], PS[psi], AF.Copy), r=[b_ps[psi]], w=[b_pSb[pk]])
                            yield
                            em("act", ACT(halo[:, c8, 0:3], pSb[pk][:, 512:515], AF.Copy), r=[b_pSb[pk]], w=[b_halo[c8]], small=True)
                            yield
                            em("dve", TS(xc, pSb[pk][:, 0:512], ppc("cw0", c8), ppc("cb", c8), ALU.mult, ALU.add), r=[b_pSb[pk], b_pp], w=[b_xc], small=True)
                            yield
                            for wj in range(1, 4):
                                em("dve", STT(xc, pSb[pk][:, wj:wj + 512], ppc("cw%d" % wj, c8), xc, ALU.mult, ALU.add), r=[b_pSb[pk], b_pp, b_xc], w=[b_xc])
                                yield
                            if c8 < 4:
                                em("act", ACT(xsT[:, c8, :], xc, AF.Silu), r=[b_xc], w=[b_xsT[c8]])
                                yield
                            else:
                                em("act", ACT(BCT[:, c8 - 4, :], xc, AF.Silu), r=[b_xc], w=[b_BCT[c8 - 4]])
                                yield
                            yield

                    for cp in range(4):
                        tsl = slice(cp * 128, (cp + 1) * 128)

                        def f_trx(e, tsl=tsl):
                            for cc in range(4):
                                ins = e.transpose(out=PT[0][:, cc * 128:(cc + 1) * 128], in_=xsT[:, cc, tsl], identity=csb("ident", 128))
                            return ins
                        em("pe", f_trx, r=b_xsT + [b_cstb], w=[b_pt[0]])
                        yield
                        em("act", ACT(xtok, PT[0].rearrange("p (a b) -> p a b", a=8), AF.Copy), r=[b_pt[0]], w=[b_xtok])
                        yield

                        def f_trb(e, tsl=tsl):
                            for g_ in range(2):
                                ins = e.transpose(out=PT[1][:, g_ * 128:(g_ + 1) * 128], in_=BCT[:, g_, tsl], identity=csb("ident", 128))
                            return ins
                        em("pe", f_trb, r=b_BCT + [b_cstb], w=[b_pt[1]])
                        yield
                        em("act", ACT(Bk, PT[1][:, 0:256], AF.Copy), r=[b_pt[1]], w=[b_Bk])
                        yield
                        yield
                        dt_bc = dtk[:, cp, :].unsqueeze(2).to_broadcast([128, 8, 64])
                        em("dve", TT(xdt, xtok, dt_bc, ALU.mult), r=[b_xtok, b_dtk], w=[b_xdt])
                        yield
                        a_cp = atk[:, cp, :]

                        def f_cs(e, cp=cp):
                            for i_, k_ in enumerate(("tri", "blk", "sel0", "sel1")):
                                e.matmul(PS[5][:, 8 * i_:8 * i_ + 8], lhsT=csb(k_, 128), rhs=ahi[:, cp, :], start=True, stop=False)
                                ins = e.matmul(PS[5][:, 8 * i_:8 * i_ + 8], lhsT=csb(k_, 128), rhs=alo[:, cp, :], start=False, stop=True)
                            return ins
                        em("pe", f_cs, r=[b_cstb, b_ahi, b_alo], w=[b_ps[5]])
                        yield
                        em("act", ACT(acs, PS[5][:, 0:32], AF.Copy), r=[b_ps[5]], w=[b_acs], small=True)
                        yield
                        em("act", ACT(E32, acs, AF.Exp), r=[b_acs], w=[b_E32], small=True)
                        yield
                        em("dve", TT(dte, acs[:, 8:16], acs[:, 0:8], ALU.subtract), r=[b_acs], w=[b_dte], small=True)
                        yield
                        em("act", ACT(dte, dte, AF.Exp), r=[b_dte], w=[b_dte], small=True)
                        yield
                        em("dve", TT(xdd, xdt, dte.unsqueeze(2).to_broadcast([128, 8, 64]), ALU.mult), r=[b_xdt, b_dte], w=[b_xdd])
                        yield
                        def f_rar(e, cp=cp):
                            e.tensor_tensor(out=rarh, in0=csb("iu", 512).rearrange("p (a b) -> p a b", a=8), in1=ahi[:, cp, :].unsqueeze(2).to_broadcast([128, 8, 64]), op=ALU.mult)
                            return e.tensor_tensor(out=rarl, in0=csb("iu", 512).rearrange("p (a b) -> p a b", a=8), in1=alo[:, cp, :].unsqueeze(2).to_broadcast([128, 8, 64]), op=ALU.mult)
                        em("dve", f_rar, r=[b_cstb, b_ahi, b_alo], w=[b_rar])
                        yield

                        def f_rarm(e):
                            e.matmul(PS[6], lhsT=csb("blk", 128), rhs=rarh.rearrange("p a b -> p (a b)"), start=True, stop=False)
                            return e.matmul(PS[6], lhsT=csb("blk", 128), rhs=rarl.rearrange("p a b -> p (a b)"), start=False, stop=True)
                        em("pe", f_rarm, r=[b_cstb, b_rar], w=[b_ps[6]])
                        yield
                        yield
                        em("dve", TT(seg, PS[6].rearrange("p (a b) -> p a b", a=8), acs[:, 0:8].unsqueeze(2).to_broadcast([128, 8, 64]), ALU.subtract),
                           r=[b_ps[6], b_acs], w=[b_seg])
                        yield
                        em("dve", TS(seg, seg, 0.0, 0.0, ALU.min, ALU.add), r=[b_seg], w=[b_seg])
                        yield
                        em("act", ACT(seg, seg, AF.Exp), r=[b_seg], w=[b_seg])
                        yield

                        def f_sc(e, cp=cp):
                            for e_ in range(2):
                                cs_ = slice((2 * cp + e_) * 64, (2 * cp + e_ + 1) * 64)
                                for g_ in range(2):
                                    ins = e.matmul(PS[5][e_ * 64:(e_ + 1) * 64, g_ * 64:(g_ + 1) * 64], lhsT=BCT[:, g_, cs_], rhs=BCT[:, 2 + g_, cs_],
                                                   start=True, stop=True, tile_position=(0, e_ * 64))
                            return ins
                        em("pe", f_sc, r=b_BCT, w=[b_ps[5]])
                        yield
                        em("dve", TT(scm, PS[5][:, 0:128].rearrange("p (a b) -> p a b", a=2), csb("iu", 128).rearrange("p (a b) -> p a b", a=2), ALU.mult),
                           r=[b_ps[5], b_cstb], w=[b_scm], small=True)
                        yield
                        yield
                        for g_ in range(2):
                            em("dve", TT(Wt[:, 4 * g_:4 * g_ + 4, :], seg[:, 4 * g_:4 * g_ + 4, :], scm[:, g_:g_ + 1, :].to_broadcast([128, 4, 64]), ALU.mult),
                               r=[b_seg, b_scm], w=[b_Wt])
                            yield

                        def mk_yd(e_):
                            def f(e):
                                es = slice(e_ * 64, (e_ + 1) * 64)
                                for h in range(8):
                                    ins = e.matmul(PS[5][es, h * 64:(h + 1) * 64], lhsT=Wt[es, h, :], rhs=xdt[es, h, :], start=True, stop=True,
                                                   tile_position=(e_ * 64, e_ * 64))
                                return ins
                            return f
                        em_pe_rt(mk_yd(0), mk_yd(1), r=[b_Wt, b_xdt], w=[b_ps[5]])
                        yield
                        em("act", ACT(yt, PS[5].rearrange("p (a b) -> p a b", a=8), AF.Copy), r=[b_ps[5]], w=[b_yt])
                        yield
                        yield
                        for e_ in range(2):
                            es = slice(e_ * 64, (e_ + 1) * 64)
                            cs_ = slice((2 * cp + e_) * 64, (2 * cp + e_ + 1) * 64)

                            def f_yo(e, es=es, cs_=cs_, e_=e_):
                                for g_ in range(2):
                                    ins = e.matmul(PS[6][es, g_ * 256:(g_ + 1) * 256], lhsT=BCT[:, 2 + g_, cs_], rhs=sstb[:, g_ * 256:(g_ + 1) * 256],
                                                   start=True, stop=True, tile_position=(0, e_ * 64))
                                return ins
                            em("pe", f_yo, r=b_BCT + [b_sstb], w=[b_ps[6]])
                            yield

                            def f_cst(e, es=es, e_=e_):
                                for h in range(8):
                                    g_ = h // 4
                                    ins = e.matmul(PS[5][:, h * 64:(h + 1) * 64], lhsT=Bk[es, g_ * 128:(g_ + 1) * 128], rhs=xdd[es, h, :],
                                                   start=True, stop=True, tile_position=(e_ * 64, 0))
                                return ins
                            em("pe", f_cst, r=[b_Bk, b_xdd], w=[b_ps[5]])
                            yield
                            cdec = E32[:, 16 + 8 * e_:24 + 8 * e_].unsqueeze(2).to_broadcast([128, 8, 64])
                            em("dve", TT(stmp, sst, cdec, ALU.mult), r=[b_sst, b_E32], w=[b_stmp])
                            yield
                            em("dve", TT(sst, stmp, PS[5].rearrange("p (a b) -> p a b", a=8), ALU.add), r=[b_stmp, b_ps[5]], w=[b_sst])
                            yield
                            em("act", ACT(sstb, sst.rearrange("p a b -> p (a b)"), AF.Copy), r=[b_sst], w=[b_sstb])
                            yield
                            yield
                        em("dve", TT(yt2, PS[6].rearrange("p (a b) -> p a b", a=8), E32[:, 0:8].unsqueeze(2).to_broadcast([128, 8, 64]), ALU.mult),
                           r=[b_ps[6], b_E32], w=[b_yt2])
                        yield
                        em("dve", TT(yt, yt, yt2, ALU.add), r=[b_yt, b_yt2], w=[b_yt])
                        yield
                        em("dve", TT(yt2, xtok, prr("dsk", 8).unsqueeze(2).to_broadcast([128, 8, 64]), ALU.mult), r=[b_xtok, b_pr], w=[b_yt2])
                        yield
                        em("dve", TT(yt, yt, yt2, ALU.add), r=[b_yt, b_yt2], w=[b_yt])
                        yield
                        em("dve", TT(yt, yt, zs[:, cp, :].rearrange("p (a b) -> p a b", a=8), ALU.mult), r=[b_yt, b_zs[cp]], w=[b_yt])
                        yield
                        yield
                        em("act", lambda e: e.activation(out=yt2.rearrange("p a b -> p (a b)"), in_=yt.rearrange("p a b -> p (a b)"), func=AF.Square,
                                                         accum_out=ssq[:, 0:1]), r=[b_yt], w=[b_yt2, b_ssq], small=True)
                        yield

                        em("act", ACT(ssq[:, 1:2], ssq[:, 0:1], AF.Ln, bias=epsc[:, 2:3], scale=1.0 / 512), r=[b_ssq, b_eps], w=[b_ssq], small=True)
                        yield
                        em("act", ACT(ssq[:, 1:2], ssq[:, 1:2], AF.Exp, scale=-0.5), r=[b_ssq], w=[b_ssq], small=True)
                        yield
                        em("dve", STT(ytb, yt.rearrange("p a b -> p (a b)"), ssq[:, 1:2], prr("sng", 512), ALU.mult, ALU.mult),
                           r=[b_yt, b_ssq, b_pr], w=[b_ytb])
                        yield

                        def f_try(e):
                            for cc in range(4):
                                ins = e.transpose(out=PT[0][:, cc * 128:(cc + 1) * 128], in_=ytb[:, cc * 128:(cc + 1) * 128], identity=csb("ident", 128))
                            return ins
                        em("pe", f_try, r=[b_ytb, b_cstb], w=[b_pt[0]])
                        yield
                        em("act", ACT(ycat[:, 4:8, tsl], PT[0].rearrange("p (a b) -> p a b", a=4), AF.Copy), r=[b_pt[0]], w=b_yc[4:8])
                        yield
                        yield


                run_streams({"rw": gen_rwkv(), "ss": gen_ssd()})
                A.release(mP)

                b_q = [None] * 8 + b_qv3
                qv = lambda c, qv3=qv3: qv3[:, c - 8, :]

                def gen_gn(ccs, T4, bT4, g0, bg0, g1, bg1, pb3):
                    t0, t1, t2, t3 = T4
                    bt0, bt1, bt2, bt3 = bT4
                    pA, pB, pC = pb3
                    for cc in ccs:
                        wcs = slice(cc * 128, (cc + 1) * 128)
                        em("act", ACT(g0, ysb[:, cc, :], AF.Copy), r=[b_ysb], w=[bg0])
                        yield
                        em("act", ACT(g1, ysb[:, cc, :], AF.Square), r=[b_ysb], w=[bg1])
                        yield
                        em("pe", lambda e, g0=g0, pA=pA: e.matmul(PS[pA], lhsT=csb("blk", 128), rhs=g0, start=True, stop=True), r=[b_cstb, bg0], w=[b_ps[pA]], cost=0.6)
                        yield
                        em("pe", lambda e, g1=g1, pB=pB: e.matmul(PS[pB], lhsT=csb("blk", 128), rhs=g1, start=True, stop=True), r=[b_cstb, bg1], w=[b_ps[pB]], cost=0.6)
                        yield
                        em("pe", lambda e, cc=cc, pC=pC: e.matmul(PS[pC], lhsT=csb("blk", 128), rhs=rkb[cc], start=True, stop=True), r=[b_cstb, b_rkb[cc]], w=[b_ps[pC]], cost=0.6)
                        yield
                        em("act", ACT(t1, PS[pA], AF.Copy, scale=1.0 / 64), r=[b_ps[pA]], w=[bt1])
                        yield
                        em("dve", TT(t3, t1, t1, ALU.mult), r=[bt1], w=[bt3])
                        yield
                        em("dve", STT(t2, PS[pB], 1.0 / 64, t3, ALU.mult, ALU.subtract), r=[b_ps[pB], bt3], w=[bt2])
                        yield

                        def f_rs2(e, t2=t2):
                            e.activation(out=t2, in_=t2, func=AF.Ln, bias=epsc[:, 1:2])
                            return e.activation(out=t2, in_=t2, func=AF.Exp, scale=-0.5)
                        em("act", f_rs2, r=[bt2, b_eps], w=[bt2], cost=1.2)
                        yield
                        em("dve", TT(t0, ysb[:, cc, :], t1, ALU.subtract), r=[b_ysb, bt1], w=[bt0])
                        yield
                        em("dve", TT(t0, t0, t2, ALU.mult), r=[bt0, bt2], w=[bt0])
                        yield
                        em("dve", TS(t0, t0, ppc("gng", cc), ppc("gnb", cc), ALU.mult, ALU.add), r=[bt0, b_pp], w=[bt0])
                        yield
                        em("dve", TT(t1, PS[pC], qv(8 + cc), ALU.mult), r=[b_ps[pC], b_q[8 + cc], bt0], w=[bt1])
                        yield
                        em("dve", TT(t0, t0, t1, ALU.add), r=[bt0, bt1], w=[bt0])
                        yield
                        em("pe", lambda e, wcs=wcs, pA=pA: e.matmul(PS[pA], lhsT=g2b[:, wcs], rhs=sgd, start=True, stop=True), r=[b_g2b, b_sgd], w=[b_ps[pA]], cost=0.6)
                        yield
                        em("dve", TT(ycat[:, cc, :], PS[pA], t0, ALU.mult), r=[b_ps[pA], bt0], w=[b_yc[cc]])
                        yield
                gsets = []
                for gi in range(2):
                    gsets.append(([A.alloc((128, 512)) for _ in range(4)], [A.buf("gt%d_%d" % (gi, i)) for i in range(4)],
                                  A.alloc((128, 512), BF16), A.buf("gs0_%d" % gi), A.alloc((128, 512), BF16), A.buf("gs1_%d" % gi)))
                def gen_oproj(ks, banks, blk):
                    for dq in range(4):
                        wi = load_w(woutv, dq * 256, 256)
                        yield
                        for sub in range(2):
                            dc = dq * 2 + sub
                            psi = banks[dc % len(banks)]
                            wc = slice(sub * 128, (sub + 1) * 128)

                            def f_o(e, wi=wi, wc=wc, psi=psi):
                                for n_, k in enumerate(ks):
                                    ins = e.matmul(PS[psi], lhsT=winb[wi][:, k, wc], rhs=ycat[:, k, :], start=(n_ == 0), stop=(n_ == len(ks) - 1))
                                return ins
                            em("pe", f_o, r=[b_win[wi]] + [b_yc[k] for k in ks], w=[b_ps[psi]], cost=1.4)
                            yield
                            xs_ = xT[:, dc, blk * 512:(blk + 1) * 512]
                            em("dve", TT(xs_, PS[psi], xs_, ALU.add), r=[b_ps[psi], b_x[dc][blk]], w=[b_x[dc][blk]])
                            yield
                run_streams({"g0": gen_gn([0, 2], *gsets[0], (0, 1, 2)), "g1": gen_gn([1, 3], *gsets[1], (4, 5, 6)),
                             "o1": gen_oproj([4, 5, 6, 7], [3], blk)})
                A.release(mR)

                if CUT < 8:
                    continue
                if DBG:
                    em("pool", lambda e, blk=blk: e.dma_start(out=dbgv[:, :, blk * 512:(blk + 1) * 512], in_=ycat), r=b_yc, w=[b_dbg], dma="ddbg")
                for _ in gen_oproj([0, 1, 2, 3], [4, 5], blk):
                    pass
            A.release(m0)

        ffn(0, "g1")
        if stage in ("full", "mix"):
            mixer()
        if stage == "full":
            ffn(1, "g2")
        sq_holder[0] = A.alloc((128, 8, 512), BF16); sq_holder[1] = [A.buf("sqf")]
        ob = [A.alloc((128, 8, 512)) for _ in range(2)]
        b_ob = [[A.buf("ob%d_%d" % (i, c)) for c in range(8)] for i in range(2)]
        b_out = [Buf("out%d" % g) for g in range(4)]
        ov = out_d.rearrange("(c p) t -> p c t", p=128)
        for g in range(4):
            oi = g % 2
            rms_group(g, "gf", lambda c, oi=oi: ob[oi][:, c, :], b_ob[oi], 6)
            em("sp", lambda e, g=g, oi=oi: e.dma_start(out=ov[:, :, g * 512:(g + 1) * 512], in_=ob[oi]),
               r=b_ob[oi], w=[b_out[g]], dma="dout%d" % oi)
        S.wait_all("sp", b_out + [b_dbg])
        S.build(nc, st)
    print("sched: inst=%d waits=%d sems=%d arena_peak=%d" % (S.ninst, S.nwait, len(S.semkeys), A.peak))
    return nc


_CACHE = {}


def kernel(**inputs):
    stage = os.environ.get("MK_STAGE", "full")
    ncores = int(os.environ.get("MK_NCORES", NCORES))
    inp = {k: np.asarray(v) for k, v in inputs.items()}
    pp = _pack_pp(inp)
    pr = _pack_pr(inp)
    cst = _consts()
    x = inp["x"].astype(np.float32, copy=False)
    common = {
        "pp": pp, "pr": pr, "cst": cst,
        "wg1": np.ascontiguousarray(inp["ffn1_w_gate"][0]), "wu1": np.ascontiguousarray(inp["ffn1_w_up"][0]),
        "wd1": np.ascontiguousarray(inp["ffn1_w_down"][0]),
        "wg2": np.ascontiguousarray(inp["ffn2_w_gate"][0]), "wu2": np.ascontiguousarray(inp["ffn2_w_up"][0]),
        "wd2": np.ascontiguousarray(inp["ffn2_w_down"][0]),
        "win": np.ascontiguousarray(inp["w_in"][0]), "wout": np.ascontiguousarray(inp["w_out"][0]),
        "w2": np.ascontiguousarray(inp["rwkv_w2"][0]), "a2": np.ascontiguousarray(inp["rwkv_a2"][0]),
        "g2": np.ascontiguousarray(inp["rwkv_g2"][0]),
    }
    key = (pp.shape[1], pr.shape[1], cst.shape[1], stage)
    nc = build_program(*key)
    in_maps = []
    for b in range(ncores):
        m = dict(common)
        m["xT"] = np.ascontiguousarray(x[b].T)
        in_maps.append(m)
    res = run_bass_kernel_spmd(nc, in_maps, core_ids=list(range(ncores)))
    kernel.last_results = res
    out = np.stack([np.ascontiguousarray(r["outT"].T) for r in res.results], axis=0)
    return out.astype(np.float32, copy=False)
```

```python
import os
import numpy as np
from contextlib import ExitStack
import concourse.bass as bass
import concourse.mybir as mybir
from concourse.bass_utils import run_bass_kernel_spmd

F32 = mybir.dt.float32
BF16 = mybir.dt.bfloat16
AF = mybir.ActivationFunctionType
ALU = mybir.AluOpType

D = 1024
L = 2048
DFF = 2816
NFC = DFF // 128
D_IN = 3336
NCORES = 8
DECAY_C = 0.6065306597126334
MODEL_LAT = float(os.environ.get("MK_LAT", "0.4"))
MODEL_PE = float(os.environ.get("MK_PE", "0.9"))
MODEL_EW = float(os.environ.get("MK_EW", "0.7"))
MODEL_RT = float(os.environ.get("MK_RT", "0.6"))


class Buf:
    __slots__ = ("name", "w", "rs")

    def __init__(self, name):
        self.name = name
        self.w = None
        self.rs = []


class Sched:
    ENGS = ("pe", "act", "dve", "pool", "sp")
    SAME_ENGINE_RAW = {"act": True, "dve": True, "pool": True, "pe": False, "sp": False}

    def __init__(self):
        self.ops = {e: [] for e in self.ENGS}
        self.cnt = {}
        self.known = {e: {} for e in self.ENGS}
        self.selfw = {e: 0 for e in self.ENGS}
        self.semkeys = []
        self.nwait = 0
        self.ninst = 0
        self.t_eng = {e: 0.0 for e in self.ENGS}
        self.t_ev = {}
        self.stream = None
        self.head = {}

    def _sem(self, key):
        if key not in self.cnt:
            self.cnt[key] = 0
            self.semkeys.append(key)

    def _merge(self, eng, ev):
        kn = self.known[eng]
        for k, v in ev[2].items():
            if kn.get(k, 0) < v:
                kn[k] = v
        if kn.get(ev[0], 0) < ev[1]:
            kn[ev[0]] = ev[1]

    def emit(self, eng, fn, reads=(), writes=(), dma=None, ninc=1, small=False, force=(), cost=None):
        deps = []
        for b in reads:
            if b.w is not None:
                deps.append((b.w, True))
        for b in writes:
            if b.w is not None:
                deps.append((b.w, False))
            for r in b.rs:
                deps.append((r, False))
        kn = self.known[eng]
        wmax = {}
        for ev, raw in deps:
            k, v = ev[0], ev[1]
            if k == eng:
                if not (raw and self.SAME_ENGINE_RAW[eng] and (small or ev[3])) or self.selfw[eng] >= v:
                    continue
                self.selfw[eng] = v
                if wmax.get(k, 0) < v:
                    wmax[k] = v
                continue
            if kn.get(k, 0) >= v:
                continue
            if wmax.get(k, 0) < v:
                wmax[k] = v
            self._merge(eng, ev)
        for ev in force:
            if wmax.get(ev[0], 0) < ev[1]:
                wmax[ev[0]] = ev[1]
        waits = list(wmax.items())
        self.nwait += len(waits)
        self.ninst += 1
        if dma is not None:
            key, step = dma, 16
        else:
            key, step = eng, 1
        self._sem(key)
        self.cnt[key] += step * ninc
        val = self.cnt[key]
        ev = (key, val, dict(kn), small)
        ready = 0.0
        for ev_d, _raw in deps:
            t_ = self.t_ev.get((ev_d[0], ev_d[1]), 0.0)
            if t_ > ready:
                ready = t_
        for ev_d in force:
            t_ = self.t_ev.get((ev_d[0], ev_d[1]), 0.0)
            if t_ > ready:
                ready = t_
        if cost is None:
            cost = 2.5 if dma is not None else (0.25 if small else (MODEL_PE if eng == "pe" else MODEL_EW))
        start = max(self.t_eng[eng], ready + MODEL_LAT)
        if dma is not None:
            self.t_eng[eng] = start + 0.1
            end = start + cost
        else:
            end = start + cost
            self.t_eng[eng] = end
        self.t_ev[(key, val)] = end
        if self.stream is not None:
            if self.head.get(self.stream, 0.0) < end:
                self.head[self.stream] = end
        if dma is None:
            kn[key] = val

        def run(e, sems, waits=waits, fn=fn, key=key, step=step):
            for k, v in waits:
                e.wait_ge(sems[k], v)
            ins = fn(e)
            if isinstance(ins, (list, tuple)):
                for i_ in ins:
                    i_.then_inc(sems[key], step)
            else:
                ins.then_inc(sems[key], step)

        self.ops[eng].append(run)
        for b in writes:
            b.w = ev
            b.rs = []
        for b in reads:
            if b not in writes:
                b.rs.append(ev)
        return ev

    def wait_all(self, eng, bufs):
        waits = {}
        for b in bufs:
            if b.w is not None:
                k, v = b.w[0], b.w[1]
                if waits.get(k, 0) < v:
                    waits[k] = v

        def run(e, sems, waits=waits):
            for k, v in waits.items():
                e.wait_ge(sems[k], v)
        self.ops[eng].append(run)

    def build(self, nc, stack):
        sems = {}
        for i, k in enumerate(self.semkeys):
            sems[k] = stack.enter_context(nc.semaphore("s%d" % i))
        block = stack.enter_context(nc.Block())
        ops = self.ops

        @block.tensor
        def _(e):
            for f in ops["pe"]:
                f(e, sems)

        @block.scalar
        def _(e):
            for f in ops["act"]:
                f(e, sems)

        @block.vector
        def _(e):
            for f in ops["dve"]:
                f(e, sems)

        @block.gpsimd
        def _(e):
            for f in ops["pool"]:
                f(e, sems)

        @block.sync
        def _(e):
            for f in ops["sp"]:
                f(e, sems)


def _cols(v):
    v = np.asarray(v, np.float32).reshape(-1, 128)
    return np.ascontiguousarray(v.T)


PP_LAYOUT = {}


def _pack_pp(inp):
    parts = [
        ("g1", _cols(inp["norm_ffn1"][0])),
        ("gm", _cols(inp["norm_mix"][0])),
        ("g2", _cols(inp["norm_ffn2"][0])),
        ("gf", _cols(inp["norm_final"])),
        ("mu", _cols(inp["rwkv_mu"][0])),
        ("w0", _cols(inp["rwkv_w0"][0])),
        ("a0", _cols(inp["rwkv_a0"][0])),
        ("kk", _cols(inp["rwkv_k_k"][0])),
        ("ka", _cols(inp["rwkv_k_a"][0])),
        ("rk", _cols(inp["rwkv_r_k"][0].reshape(-1))),
        ("gng", _cols(inp["rwkv_gn_g"][0])),
        ("gnb", _cols(inp["rwkv_gn_b"][0])),
        ("cw0", _cols(inp["ssm_conv_w"][0][0])),
        ("cw1", _cols(inp["ssm_conv_w"][0][1])),
        ("cw2", _cols(inp["ssm_conv_w"][0][2])),
        ("cw3", _cols(inp["ssm_conv_w"][0][3])),
        ("cb", _cols(inp["ssm_conv_b"][0])),
    ]
    off = 0
    for k, a in parts:
        PP_LAYOUT[k] = off
        off += a.shape[1]
    return np.ascontiguousarray(np.concatenate([a for _, a in parts], axis=1))


PR_LAYOUT = {}


def _pack_pr(inp):
    rep = lambda v: np.ascontiguousarray(np.broadcast_to(np.asarray(v, np.float32).reshape(1, -1), (128, np.asarray(v).size)))
    parts = [
        ("dtb", rep(inp["ssm_dt_bias"][0])),
        ("alog", rep(inp["ssm_a_log"][0])),
        ("dsk", rep(inp["ssm_d"][0])),
        ("sng", rep(inp["ssm_norm"][0])),
    ]
    off = 0
    for k, a in parts:
        PR_LAYOUT[k] = off
        off += a.shape[1]
    return np.ascontiguousarray(np.concatenate([a for _, a in parts], axis=1))


CONST_LAYOUT = {}


def _consts():
    p = np.arange(128)
    e = p // 64
    j = p % 64
    i64 = np.arange(64)
    su = (j[:, None] < i64[None, :]).astype(np.float32)
    iu = (j[:, None] <= i64[None, :]).astype(np.float32)
    sl = (j[:, None] > i64[None, :]).astype(np.float32)
    eye = (j[:, None] == i64[None, :]).astype(np.float32)
    t512 = np.arange(512)
    parts = [
        ("iu", np.tile(iu, (1, 8))),
        ("blk", (e[:, None] == e[None, :]).astype(np.float32)),
        ("tri", ((e[:, None] == e[None, :]) & (j[:, None] <= j[None, :])).astype(np.float32)),
        ("sel0", np.broadcast_to((e[:, None] == 0), (128, 128)).astype(np.float32)),
        ("sel1", np.broadcast_to((e[:, None] == 1), (128, 128)).astype(np.float32)),
        ("rst", np.broadcast_to((t512 % 64 != 0)[None, :], (128, 512)).astype(np.float32)),
        ("ident", np.eye(128, dtype=np.float32)),
        ("su", np.tile(su, (1, 8))),
        ("sl", np.tile(sl, (1, 8))),
        ("eye", np.tile(eye, (1, 8))),
        ("ones", np.ones((128, 128), np.float32)),
    ]
    off = 0
    for k, a in parts:
        CONST_LAYOUT[k] = off
        off += a.shape[1]
    return np.ascontiguousarray(np.concatenate([a for _, a in parts], axis=1))


class Arena:
    def __init__(self, t, nbytes):
        self.t = t
        self.n = nbytes
        self.off = 0
        self.peak = 0
        self.live = []
        self.pending = {}

    def alloc(self, shape, dt=F32):
        nel = 1
        for d_ in shape[1:]:
            nel *= d_
        esz = 4 if dt == F32 else 2
        nb = (nel * esz + 63) // 64 * 64
        o = self.off
        self.off += nb
        self.peak = max(self.peak, self.off)
        assert self.off <= self.n, "arena overflow: %d > %d" % (self.off, self.n)
        v = self.t[:, o // 4:(o + nb) // 4]
        if dt != F32:
            v = v.bitcast(dt)
        v = v[:, :nel]
        if len(shape) == 3:
            v = v.rearrange("p (a b) -> p a b", a=shape[1])
        return v

    def buf(self, name):
        b = Buf(name)
        b.rs = list(self.pending.values())
        self.live.append((self.off, b))
        return b

    def mark(self):
        return self.off

    def release(self, m):
        keep = []
        for o, b in self.live:
            if o >= m:
                evs = list(b.rs)
                if b.w is not None:
                    evs.append(b.w)
                for ev in evs:
                    cur = self.pending.get(ev[0])
                    if cur is None or cur[1] < ev[1]:
                        self.pending[ev[0]] = ev
            else:
                keep.append((o, b))
        self.live = keep
        self.off = m


def build_program(npp, npr, ncst, stage="full"):
    CUT = int(os.environ.get("MK_CUT", "99"))
    nc = bass.Bass("TRN2", target_bir_lowering=False)
    dt_in = lambda name, shape: nc.dram_tensor(name, list(shape), F32, kind="ExternalInput").ap()
    xT_d = dt_in("xT", (D, L))
    pp_d = dt_in("pp", (128, npp))
    pr_d = dt_in("pr", (128, npr))
    cst_d = dt_in("cst", (128, ncst))
    wg_d = [dt_in("wg1", (D, DFF)), dt_in("wg2", (D, DFF))]
    wu_d = [dt_in("wu1", (D, DFF)), dt_in("wu2", (D, DFF))]
    wd_d = [dt_in("wd1", (DFF, D)), dt_in("wd2", (DFF, D))]
    win_d = dt_in("win", (D, D_IN))
    wout_d = dt_in("wout", (D, D))
    w2_d = dt_in("w2", (64, 512))
    a2_d = dt_in("a2", (64, 512))
    g2_d = dt_in("g2", (128, 512))
    out_d = nc.dram_tensor("outT", [D, L], F32, kind="ExternalOutput").ap()
    DBG = bool(int(os.environ.get("MK_DBG", "0")))
    if DBG:
        dbg_d = nc.dram_tensor("dbgy", [D, L], F32, kind="ExternalOutput").ap()
        dbgv = dbg_d.rearrange("(c p) t -> p c t", p=128)
    b_dbg = Buf("dbg")

    S = Sched()
    with ExitStack() as st:
        try:
            st.enter_context(nc.allow_low_precision("bf16 matmul operands by design"))
        except Exception:
            pass
        ARENA_BYTES = 207 * 1024
        arena_t = st.enter_context(nc.sbuf_tensor("arena", [128, ARENA_BYTES // 4], F32))
        A = Arena(arena_t, ARENA_BYTES)
        psum = lambda name, shape, dt=F32: st.enter_context(nc.psum_tensor(name, list(shape), dt))

        def em(eng, fn, r=(), w=(), **kw):
            return S.emit(eng, fn, reads=r, writes=w, **kw)

        def run_streams(gens):
            t_now = max(S.t_eng.values())
            S.head = {k: t_now for k in gens}
            gens = dict(gens)
            while gens:
                sid = min(gens, key=lambda k: S.head[k])
                S.stream = sid
                try:
                    r_ = next(gens[sid])
                    if r_ == "blocked":
                        others = [S.head[k] for k in gens if k != sid]
                        S.head[sid] = (max(others) if others else S.head[sid]) + 1e-3
                except StopIteration:
                    del gens[sid]
            S.stream = None

        def em_pe_rt(fA, fB, r=(), w=()):
            evA = S.emit("pe", fA, reads=r, writes=w, cost=MODEL_RT)
            return S.emit("pe", fB, reads=r, writes=w, force=[evA], cost=MODEL_RT)

        ACT = lambda out, in_, func, **kw: (lambda e: e.activation(out=out, in_=in_, func=func, **kw))
        TT = lambda out, a, b, op: (lambda e: e.tensor_tensor(out=out, in0=a, in1=b, op=op))
        TS = lambda out, a, s1, s2, op0, op1: (lambda e: e.tensor_scalar(out=out, in0=a, scalar1=s1, scalar2=s2, op0=op0, op1=op1))
        STT = lambda out, a, s, b, op0, op1: (lambda e: e.scalar_tensor_tensor(out=out, in0=a, scalar=s, in1=b, op0=op0, op1=op1))
        TCP = lambda out, in_: (lambda e: e.tensor_copy(out=out, in_=in_))

        xT = A.alloc((128, 8, L))
        b_x = [[A.buf("x%d_%d" % (c, g)) for g in range(4)] for c in range(8)]
        pp = A.alloc((128, npp)); b_pp = A.buf("pp")
        pr = A.alloc((128, npr)); b_pr = A.buf("pr")
        NCF = 512
        cstf = A.alloc((128, NCF)); b_cstf = A.buf("cstf")
        cstb = A.alloc((128, ncst), BF16); b_cstb = A.buf("cstb")
        epsc = A.alloc((128, 8)); b_eps = A.buf("eps")
        PS = [psum("ps%d" % i, (128, 512))[:, :] for i in range(7)]
        b_ps = [Buf("ps%d" % i) for i in range(7)]
        PTt = psum("ptt", (128, 1024), BF16)[:, :]
        PT = [PTt[:, i * 512:(i + 1) * 512] for i in range(2)]
        b_pt = [Buf("pt")] * 2

        CL = CONST_LAYOUT
        csf = lambda k, n: cstf[:, CL[k] - CL["rst"]:CL[k] - CL["rst"] + n]
        csb = lambda k, n: cstb[:, CL[k]:CL[k] + n]
        ppc = lambda k, c: pp[:, PP_LAYOUT[k] + c:PP_LAYOUT[k] + c + 1]
        prr = lambda k, n: pr[:, PR_LAYOUT[k]:PR_LAYOUT[k] + n]

        omm = A.alloc((128, 14)); b_omm = A.buf("omm")
        carry = A.alloc((128, 14)); b_carry = [A.buf("carry%d" % c) for c in range(14)]
        halo = A.alloc((128, 8, 4)); b_halo = [A.buf("halo%d" % c) for c in range(8)]
        Tst = [A.alloc((128, 4, 64)) for _ in range(2)]; b_T = [A.buf("T0"), A.buf("T1")]
        Tbf = [A.alloc((128, 4, 64), BF16) for _ in range(2)]; b_Tb = [A.buf("Tb0"), A.buf("Tb1")]
        sst = A.alloc((128, 8, 64)); b_sst = A.buf("sst")
        sstb = A.alloc((128, 512), BF16); b_sstb = A.buf("sstb")
        Arow = A.alloc((128, 8)); b_Arow = A.buf("Arow")
        w2b = A.alloc((128, 512), BF16); b_w2b = A.buf("w2b")
        g2b = A.alloc((128, 512), BF16); b_g2b = A.buf("g2b")

        xv = xT_d.rearrange("(c p) t -> p c t", p=128)
        for g in range(4):
            em("sp", lambda e, g=g: e.dma_start(out=xT[:, :, g * 512:(g + 1) * 512], in_=xv[:, :, g * 512:(g + 1) * 512]),
               w=[b_x[c][g] for c in range(8)], dma="dx%d" % g)
        em("sp", lambda e: e.dma_start(out=pp, in_=pp_d), w=[b_pp], dma="dc0")
        em("sp", lambda e: e.dma_start(out=pr, in_=pr_d), w=[b_pr], dma="dc1")
        em("sp", lambda e: e.dma_start(out=cstf, in_=cst_d[:, CONST_LAYOUT["rst"]:CONST_LAYOUT["rst"] + NCF]), w=[b_cstf], dma="dc2")
        em("pool", lambda e: e.dma_start(out=cstb, in_=cst_d), w=[b_cstb], dma="dc3")
        em("pool", lambda e: [e.dma_start(out=w2b[0:64, :], in_=w2_d), e.dma_start(out=w2b[64:128, :], in_=a2_d)],
           w=[b_w2b], dma="dc4", ninc=2)
        em("pool", lambda e: e.dma_start(out=g2b, in_=g2_d), w=[b_g2b], dma="dc5")

        def f_eps(e):
            e.memset(epsc[:, 0:1], 1e-6)
            e.memset(epsc[:, 1:2], 64e-5)
            e.memset(epsc[:, 2:3], 1e-5)
            e.memset(epsc[:, 3:4], 1.0)
            e.memset(epsc[:, 4:5], 1e-24)
            e.memset(carry, 0.0)
            e.memset(halo, 0.0)
            e.memset(Tst[0], 0.0)
            e.memset(Tbf[0], 0.0)
            e.memset(sstb, 0.0)
            return e.memset(sst, 0.0)
        em("dve", f_eps, w=[b_eps, b_T[0], b_Tb[0], b_sst, b_sstb] + b_carry + b_halo)
        em("dve", TS(omm, pp[:, PP_LAYOUT["mu"]:PP_LAYOUT["mu"] + 14], -1.0, 1.0, ALU.mult, ALU.add), r=[b_pp], w=[b_omm], small=True)

        def f_arow(e):
            return e.activation(out=Arow, in_=prr("alog", 8), func=AF.Exp)
        em("act", f_arow, r=[b_pr], w=[b_Arow], small=True)
        em("dve", TS(Arow, Arow, -1.0, 0.0, ALU.mult, ALU.add), r=[b_Arow], w=[b_Arow], small=True)

        sq_holder = [None, None]
        rstd = [A.alloc((128, 512))] * 2; b_rstd = [A.buf("rstd0")] * 2
        norm_ctr = [0]

        def rms_group(g, gkey, out_fn, out_bufs, psi):
            sq, bsq_l = sq_holder
            tok = slice(g * 512, (g + 1) * 512)
            k = norm_ctr[0] % 2
            norm_ctr[0] += 1
            em("act", ACT(sq, xT[:, :, tok], AF.Square), r=[b_x[c][g] for c in range(8)], w=bsq_l)

            def f_mm(e):
                for c in range(8):
                    ins = e.matmul(PS[psi], lhsT=csb("ones", 128), rhs=sq[:, c, :], start=(c == 0), stop=(c == 7))
                return ins
            em("pe", f_mm, r=list(bsq_l) + [b_cstb], w=[b_ps[psi]])

            def f_rs(e):
                e.activation(out=rstd[k], in_=PS[psi], func=AF.Ln, bias=epsc[:, 0:1], scale=1.0 / D)
                return e.activation(out=rstd[k], in_=rstd[k], func=AF.Exp, scale=-0.5)
            em("act", f_rs, r=[b_ps[psi], b_eps], w=[b_rstd[k]])
            for c in range(8):
                em("dve", STT(out_fn(c), xT[:, c, tok], ppc(gkey, c), rstd[k], ALU.mult, ALU.mult),
                   r=[b_x[c][g], b_pp, b_rstd[k]], w=[out_bufs[c]])

        def ffn(fi, gkey):
            m0 = A.mark()
            sq_holder[0] = A.alloc((128, 8, 512), BF16); sq_holder[1] = [A.buf("sq")]
            hT = A.alloc((128, 8, 1024), BF16)
            b_h = [[A.buf("h%d_%d" % (c, g)) for g in range(2)] for c in range(8)]
            actT = A.alloc((128, NFC, 1024), BF16)
            b_act = [[A.buf("a%d_%d" % (f, g)) for g in range(2)] for f in range(NFC)]
            NWB = 2
            wgb = [A.alloc((128, 8, 256), BF16) for i in range(NWB)]; b_wg = [A.buf("wg%d" % i) for i in range(NWB)]
            wub = [A.alloc((128, 8, 256), BF16) for i in range(NWB)]; b_wu = [A.buf("wu%d" % i) for i in range(NWB)]
            wdb = [A.alloc((128, NFC, 256), BF16) for i in range(2)]; b_wd = [A.buf("wd%d" % i) for i in range(2)]
            sgt = [A.alloc((128, 512)) for i in range(2)]; b_sg = [A.buf("sg0"), A.buf("sg1")]
            wgv = wg_d[fi].rearrange("(k p) c -> p k c", p=128)
            wuv = wu_d[fi].rearrange("(k p) c -> p k c", p=128)
            wdv = wd_d[fi].rearrange("(f p) c -> p f c", p=128)
            wq = 0
            dq_ctr = 0
            for hf in range(2):
                for g2 in range(2):
                    g = hf * 2 + g2
                    rms_group(g, gkey, lambda c, g2=g2: hT[:, c, g2 * 512:(g2 + 1) * 512], [b_h[c][g2] for c in range(8)], 6)
                for jb in range(DFF // 256):
                    wi = wq % NWB
                    wq += 1
                    cs_ = slice(jb * 256, (jb + 1) * 256)
                    em("pool", lambda e, wi=wi, cs_=cs_: e.dma_start(out=wgb[wi], in_=wgv[:, :, cs_]),
                       w=[b_wg[wi]], dma="dwg%d" % wi)
                    em("pool", lambda e, wi=wi, cs_=cs_: e.dma_start(out=wub[wi], in_=wuv[:, :, cs_]),
                       w=[b_wu[wi]], dma="dwu%d" % wi)
                    for sub in range(2):
                        fc = jb * 2 + sub
                        for g2 in range(2):
                            it = (fc * 2 + g2)
                            pg, pu = (it % 2) * 2, (it % 2) * 2 + 1
                            tk = slice(g2 * 512, (g2 + 1) * 512)
                            wc = slice(sub * 128, (sub + 1) * 128)

                            def f_g(e, wi=wi, wc=wc, tk=tk, pg=pg):
                                for k in range(8):
                                    ins = e.matmul(PS[pg], lhsT=wgb[wi][:, k, wc], rhs=hT[:, k, tk], start=(k == 0), stop=(k == 7))
                                return ins
                            em("pe", f_g, r=[b_wg[wi]] + [b_h[c][g2] for c in range(8)], w=[b_ps[pg]])

                            def f_u(e, wi=wi, wc=wc, tk=tk, pu=pu):
                                for k in range(8):
                                    ins = e.matmul(PS[pu], lhsT=wub[wi][:, k, wc], rhs=hT[:, k, tk], start=(k == 0), stop=(k == 7))
                                return ins
                            em("pe", f_u, r=[b_wu[wi]] + [b_h[c][g2] for c in range(8)], w=[b_ps[pu]])
                            si = it % 2
                            em("act", ACT(sgt[si], PS[pg], AF.Silu), r=[b_ps[pg]], w=[b_sg[si]])
                            em("dve", TT(actT[:, fc, tk], PS[pu], sgt[si], ALU.mult),
                               r=[b_ps[pu], b_sg[si]], w=[b_act[fc][g2]])
                for dq in range(4):
                    wi = dq_ctr % 2
                    dq_ctr += 1
                    cs_ = slice(dq * 256, (dq + 1) * 256)
                    em("pool", lambda e, wi=wi, cs_=cs_: [
                        e.dma_start(out=wdb[wi][:, 0:11, :], in_=wdv[:, 0:11, cs_]),
                        e.dma_start(out=wdb[wi][:, 11:22, :], in_=wdv[:, 11:22, cs_])],
                        w=[b_wd[wi]], dma="dwd%d" % wi, ninc=2)
                    for sub in range(2):
                        dc = dq * 2 + sub
                        for g2 in range(2):
                            g = hf * 2 + g2
                            pi = 4 + (dc * 2 + g2) % 2
                            tk = slice(g2 * 512, (g2 + 1) * 512)
                            wc = slice(sub * 128, (sub + 1) * 128)

                            def f_d(e, wi=wi, wc=wc, tk=tk, pi=pi):
                                for f in range(NFC):
                                    ins = e.matmul(PS[pi], lhsT=wdb[wi][:, f, wc], rhs=actT[:, f, tk], start=(f == 0), stop=(f == NFC - 1))
                                return ins
                            em("pe", f_d, r=[b_wd[wi]] + [b_act[f][g2] for f in range(NFC)], w=[b_ps[pi]])
                            xs_ = xT[:, dc, g * 512:(g + 1) * 512]
                            em("dve", STT(xs_, PS[pi], 0.5, xs_, ALU.mult, ALU.add),
                               r=[b_ps[pi], b_x[dc][g]], w=[b_x[dc][g]])
            A.release(m0)

        def mixer():
            m0 = A.mark()
            hTb = A.alloc((128, 8, 512), BF16); b_hb = [A.buf("hb%d" % c) for c in range(8)]
            NW = 3
            winb = [A.alloc((128, 8, 256), BF16) for i in range(NW)]; b_win = [A.buf("win%d" % i) for i in range(NW)]
            ycat = A.alloc((128, 8, 512), BF16); b_yc = [A.buf("yc%d" % c) for c in range(8)]
            sq_holder[0] = ycat; sq_holder[1] = b_yc
            winv = win_d.rearrange("(k p) c -> p k c", p=128)
            woutv = wout_d.rearrange("(k p) c -> p k c", p=128)
            wctr = [0]

            def load_w(view, c0, ncol):
                wi = wctr[0] % NW
                wctr[0] += 1
                em("pool", lambda e, wi=wi: e.dma_start(out=winb[wi][:, :, 0:ncol], in_=view[:, :, c0:c0 + ncol]),
                   w=[b_win[wi]], dma="dwin%d" % wi)
                return wi

            def proj_fm(wi, sub, psi):
                wc = slice(sub * 128, (sub + 1) * 128)

                def f(e):
                    for k in range(8):
                        ins = e.matmul(PS[psi], lhsT=winb[wi][:, k, wc], rhs=hTb[:, k, :], start=(k == 0), stop=(k == 7))
                    return ins
                em("pe", f, r=[b_win[wi]] + b_hb, w=[b_ps[psi]], cost=2.5)

            for blk in range(4):
                rms_group(blk, "gm", lambda c: hTb[:, c, :], b_hb, 6)
                mR = A.mark()
                qv3 = A.alloc((128, 4, 512)); b_qv3 = [A.buf("qv%d" % c) for c in range(4)]
                tmu = [A.alloc((128, 512)) for _ in range(2)]; b_tmu = [A.buf("tmu0"), A.buf("tmu1")]
                sgd = A.alloc((128, 512), BF16); b_sgd = A.buf("sgd")
                AT = [A.alloc((128, 512), BF16) for _ in range(4)]; b_AT = [A.buf("AT%d" % i) for i in range(4)]
                RT = [A.alloc((128, 512), BF16) for _ in range(4)]; b_RT = [A.buf("RT%d" % i) for i in range(4)]
                BT = [A.alloc((128, 512), BF16) for _ in range(4)]; b_BT = [A.buf("BT%d" % i) for i in range(4)]
                KT = [A.alloc((128, 512), BF16) for _ in range(4)]; b_KT = [A.buf("KT%d" % i) for i in range(4)]
                VB = [A.alloc((128, 512), BF16) for _ in range(4)]; b_VB = [A.buf("VB%d" % i) for i in range(4)]
                rkb = [A.alloc((128, 512), BF16) for _ in range(4)]; b_rkb = [A.buf("rkb%d" % i) for i in range(4)]
                dWt = A.alloc((128, 4, 8)); b_dW = [A.buf("dW%d" % i) for i in range(4)]
                Ktok = [A.alloc((128, 512), BF16) for _ in range(1)]; b_Ktok = [A.buf("Ktok%d" % i) for i in range(2)]
                Btok = [A.alloc((128, 512), BF16) for _ in range(1)]; b_Btok = [A.buf("Btok%d" % i) for i in range(2)]
                Vtok = [A.alloc((128, 512), BF16) for _ in range(1)]; b_Vtok = [A.buf("Vtok%d" % i) for i in range(2)]
                Utok = [A.alloc((128, 512), BF16) for _ in range(1)]; b_Utok = [[A.buf("Utok%d_%d" % (i, e_)) for e_ in range(2)] for i in range(2)]
                MT = [A.alloc((128, 512), BF16) for _ in range(1)]; b_MT = [A.buf("MT%d" % i) for i in range(2)]
                AakT = [A.alloc((128, 512), BF16) for _ in range(1)]; b_Aak = [A.buf("Aak%d" % i) for i in range(2)]
                ArbT = [A.alloc((128, 512), BF16) for _ in range(1)]; b_Arb = [A.buf("Arb%d" % i) for i in range(2)]
                ArkT = [A.alloc((128, 512), BF16) for _ in range(1)]; b_Ark = [A.buf("Ark%d" % i) for i in range(2)]
                Qa = [A.alloc((128, 512), BF16) for _ in range(2)]; b_Qa = [A.buf("Qa0"), A.buf("Qa1")]
                Qt = [A.alloc((128, 512), BF16) for _ in range(2)]; b_Qt = [A.buf("Qt0"), A.buf("Qt1")]
                Pm = [A.alloc((128, 512), BF16) for _ in range(2)]; b_Pm = [A.buf("Pm0"), A.buf("Pm1")]
                Zb = A.alloc((128, 512), BF16); b_Zb = [A.buf("Zb0"), A.buf("Zb1")]
                ysb = A.alloc((128, 4, 512)); b_ysb = A.buf("ysb")
                Ttmp = tmu[0][:, 0:256].rearrange("p (a b) -> p a b", a=4); b_Ttmp = b_tmu[0]
                mP = A.mark()
                qrk = A.alloc((128, 8, 512)); b_qrk = [A.buf("q%d" % c) for c in range(8)]
                tw = A.alloc((128, 512), BF16); b_tw = A.buf("tw")
                tt_ = [A.alloc((128, 512)) for _ in range(4)]; b_t = [A.buf("t%d" % i) for i in range(4)]
                s0 = A.alloc((128, 512), BF16); b_s0 = A.buf("s0")
                s1 = A.alloc((128, 512), BF16); b_s1 = A.buf("s1")
                b_q = b_qrk + b_qv3 + [b_t[2], b_t[3]]
                qv = lambda c, qrk=qrk, tt_=tt_, qv3=qv3: (qrk[:, c, :] if c < 8 else (qv3[:, c - 8, :] if c < 12 else tt_[c - 10]))

                done_chunks = set()

                def gen_inproj():
                    order = [6, 0, 2, 4, 1, 3, 5]
                    pctr = 0
                    for jb in order:
                        wi = load_w(winv, jb * 256, 256)
                        for sub in range(2):
                            c = jb * 2 + sub
                            psi = pctr % 2
                            tm = pctr % 2
                            pctr += 1
                            proj_fm(wi, sub, psi)
                            yield
                            em("act", ACT(tmu[tm], PS[psi], AF.Copy, scale=ppc("mu", c)), r=[b_ps[psi], b_pp], w=[b_tmu[tm]])
                            yield
                            em("dve", STT(qv(c)[:, 1:512], PS[psi][:, 1:512], omm[:, c:c + 1], tmu[tm][:, 0:511], ALU.mult, ALU.add),
                               r=[b_ps[psi], b_omm, b_tmu[tm]], w=[b_q[c]])
                            yield
                            em("dve", STT(qv(c)[:, 0:1], PS[psi][:, 0:1], omm[:, c:c + 1], carry[:, c:c + 1], ALU.mult, ALU.add),
                               r=[b_ps[psi], b_omm, b_carry[c]], w=[b_q[c]], small=True)
                            yield
                            em("act", ACT(carry[:, c:c + 1], tmu[tm][:, 511:512], AF.Copy), r=[b_tmu[tm]], w=[b_carry[c]], small=True)
                            yield
                            done_chunks.add(c)
                        if jb == 6:
                            def f_tw(e):
                                e.activation(out=tw[0:64, :], in_=tt_[2][0:64, :], func=AF.Tanh)
                                return e.activation(out=tw[64:128, :], in_=tt_[2][64:128, :], func=AF.Copy)
                            em("act", f_tw, r=[b_q[12]], w=[b_tw])
                            yield
                            em("act", ACT(sgd, tt_[3], AF.Sigmoid), r=[b_q[13]], w=[b_sgd])
                            yield
                            done_chunks.add("tw")


                def gen_prep(ccs, T4, bT4, s0_, bs0_, pw, pa, pb):
                    for cc in ccs:
                        while not {"tw", cc, 4 + cc, 8 + cc} <= done_chunks:
                            yield "blocked"
                        wcs = slice(cc * 128, (cc + 1) * 128)
                        r_ = qv(cc)
                        k_ = qv(4 + cc)
                        t0, t1, t2, t3 = T4
                        bt0, bt1, bt2, bt3 = bT4
                        em("pe", lambda e, wcs=wcs, pw=pw: e.matmul(PS[pw], lhsT=w2b[0:64, wcs], rhs=tw[0:64, :], start=True, stop=True, tile_position=(0, 0)),
                           r=[b_w2b, b_tw], w=[b_ps[pw]])
                        yield
                        em("pe", lambda e, wcs=wcs, pa=pa: e.matmul(PS[pa], lhsT=w2b[64:128, wcs], rhs=tw[64:128, :], start=True, stop=True, tile_position=(64, 0)),
                           r=[b_w2b, b_tw], w=[b_ps[pa]])
                        yield
                        em("act", ACT(t0, PS[pw], AF.Sigmoid, bias=ppc("w0", cc)), r=[b_ps[pw], b_pp], w=[bt0])
                        yield
                        em("dve", lambda e, t0=t0, t1=t1: e.tensor_tensor_scan(out=t1, data0=csf("rst", 512), data1=t0, initial=0.0, op0=ALU.mult, op1=ALU.add),
                           r=[bt0, b_cstf], w=[bt1])
                        yield
                        em("dve", TT(t2, t1, t0, ALU.subtract), r=[bt1, bt0], w=[bt2])
                        yield
                        em("act", ACT(t2, t2, AF.Exp, scale=-DECAY_C), r=[bt2], w=[bt2])
                        yield
                        em("act", ACT(t3, t1, AF.Exp, scale=-DECAY_C), r=[bt1], w=[bt3])
                        yield
                        em("act", ACT(t1, t1, AF.Exp, scale=DECAY_C), r=[bt1], w=[bt1])
                        yield
                        em("act", ACT(dWt[:, cc, :], t3.rearrange("p (c j) -> p c j", j=64)[:, :, 63], AF.Copy), r=[bt3], w=[b_dW[cc]], small=True)
                        yield
                        em("dve", TT(RT[cc], r_, t3, ALU.mult), r=[b_q[cc], bt3], w=[b_RT[cc]])
                        yield
                        em("dve", TS(t3, k_, ppc("kk", cc), 0.0, ALU.mult, ALU.add), r=[b_q[4 + cc], b_pp, b_RT[cc]], w=[bt3])
                        yield
                        em("act", ACT(s0_, t3, AF.Square), r=[bt3], w=[bs0_])
                        yield
                        em("pe", lambda e, s0_=s0_, pb=pb: e.matmul(PS[pb], lhsT=csb("blk", 128), rhs=s0_, start=True, stop=True), r=[b_cstb, bs0_], w=[b_ps[pb]])
                        yield

                        def f_rn(e, t0=t0, pb=pb):
                            e.activation(out=t0, in_=PS[pb], func=AF.Ln, bias=epsc[:, 4:5])
                            return e.activation(out=t0, in_=t0, func=AF.Exp, scale=-0.5)
                        em("act", f_rn, r=[b_ps[pb], b_eps, bt2], w=[bt0])
                        yield
                        em("dve", TT(t3, t3, t0, ALU.mult), r=[bt3, bt0], w=[bt3])
                        yield
                        em("dve", STT(AT[cc], t3, -1.0, t2, ALU.mult, ALU.mult), r=[bt3, bt2], w=[b_AT[cc]])
                        yield
                        em("act", ACT(t0, PS[pa], AF.Sigmoid, bias=ppc("a0", cc)), r=[b_ps[pa], b_pp, bt3], w=[bt0])
                        yield
                        em("dve", TT(t2, t3, t0, ALU.mult), r=[bt3, bt0, b_AT[cc]], w=[bt2])
                        yield
                        em("dve", TT(BT[cc], t2, t1, ALU.mult), r=[bt2, bt1], w=[b_BT[cc]])
                        yield
                        em("dve", TS(t2, t0, -1.0, ppc("ka", cc), ALU.add, ALU.mult), r=[bt0, b_pp, b_BT[cc]], w=[bt2])
                        yield
                        em("dve", STT(t2, t2, 1.0, k_, ALU.add, ALU.mult), r=[bt2, b_q[4 + cc]], w=[bt2])
                        yield
                        em("dve", TT(KT[cc], t2, t1, ALU.mult), r=[bt2, bt1], w=[b_KT[cc]])
                        yield
                        em("dve", STT(rkb[cc], r_, ppc("rk", cc), t2, ALU.mult, ALU.mult), r=[b_q[cc], b_pp, bt2], w=[b_rkb[cc]])
                        yield
                        em("act", ACT(VB[cc], qv(8 + cc), AF.Copy), r=[b_q[8 + cc]], w=[b_VB[cc]])
                        yield


                tB = [ysb[:, i_, :] for i_ in range(4)]; b_tB = [A.buf("tB%d" % i_) for i_ in range(4)]
                s0B = A.alloc((128, 512), BF16); b_s0B = A.buf("s0B")
                run_streams({"ip": gen_inproj(), "pA": gen_prep([0, 2], tt_, b_t, s0, b_s0, 4, 5, 6), "pB": gen_prep([1, 3], tB, b_tB, s0B, b_s0B, 2, 3, 2)})
                A.release(mP)

                pSb = [A.alloc((128, 516)) for _ in range(2)]; b_pSb = [A.buf("pSb0"), A.buf("pSb1")]
                xc = A.alloc((128, 512)); b_xc = A.buf("xc")
                xsT = A.alloc((128, 4, 512), BF16); b_xsT = [A.buf("xsT%d" % c) for c in range(4)]
                BCT = A.alloc((128, 4, 512), BF16); b_BCT = [A.buf("BCT%d" % c) for c in range(4)]
                zs = A.alloc((128, 4, 512), BF16); b_zs = [A.buf("zs%d" % c) for c in range(4)]
                dtk = A.alloc((128, 4, 8)); b_dtk = A.buf("dtk")
                atk = A.alloc((128, 4, 8)); b_atk = A.buf("atk")
                xtok = A.alloc((128, 8, 64), BF16); b_xtok = A.buf("xtok")
                Bk = A.alloc((128, 256), BF16); b_Bk = A.buf("Bk")
                xdt = A.alloc((128, 8, 64), BF16); b_xdt = A.buf("xdt")
                xdd = A.alloc((128, 8, 64), BF16); b_xdd = A.buf("xdd")
                acs = A.alloc((128, 32)); b_acs = A.buf("acs")
                E32 = A.alloc((128, 32)); b_E32 = A.buf("E32")
                dte = A.alloc((128, 8)); b_dte = A.buf("dte")
                rarh = A.alloc((128, 8, 64), BF16); rarl = A.alloc((128, 8, 64), BF16); b_rar = A.buf("rar")
                ahi = A.alloc((128, 4, 8), BF16); alo = A.alloc((128, 4, 8), BF16); atmp = A.alloc((128, 4, 8))
                b_ahi = A.buf("ahi"); b_alo = A.buf("alo"); b_atmp = A.buf("atmp")
                seg = A.alloc((128, 8, 64)); b_seg = A.buf("seg")
                scm = A.alloc((128, 2, 64)); b_scm = A.buf("scm")
                Wt = A.alloc((128, 8, 64), BF16); b_Wt = A.buf("Wt")
                yt = A.alloc((128, 8, 64)); b_yt = A.buf("yt")
                yt2 = A.alloc((128, 8, 64)); b_yt2 = A.buf("yt2")
                stmp = yt2; b_stmp = b_yt2
                ssq = A.alloc((128, 2)); b_ssq = A.buf("ssq")
                ytb = A.alloc((128, 512), BF16); b_ytb = A.buf("ytb")

                def gen_rwkv():
                    v4 = lambda ap: ap.rearrange("p (m t) -> p m t", t=128)
                    for cp in range(4):
                        pb = 0
                        tsl = slice(cp * 128, (cp + 1) * 128)
                        for (src, bsrc, dst, bdst, pti) in ((KT, b_KT, Ktok, b_Ktok, 0), (BT, b_BT, Btok, b_Btok, 1), (VB, b_VB, Vtok, b_Vtok, 0)):
                            def f_tr(e, src=src, pti=pti, tsl=tsl):
                                for cc in range(4):
                                    ins = e.transpose(out=PT[pti][:, cc * 128:(cc + 1) * 128], in_=src[cc][:, tsl], identity=csb("ident", 128))
                                return ins
                            em("pe", f_tr, r=list(bsrc) + [b_cstb], w=[b_pt[pti]])
                            yield
                            em("act", ACT(dst[pb], PT[pti], AF.Copy), r=[b_pt[pti]], w=[bdst[pb]])
                            yield

                        yield

                        def amat(px, lt, rt, blt, brt, dstt, bdst, mask):
                            def mk(par):
                                def f(e, cp=cp):
                                    for h in range(par, 8, 2):
                                        cc, po = h // 2, 64 * par
                                        for e_ in range(2):
                                            cs_ = slice((2 * cp + e_) * 64, (2 * cp + e_ + 1) * 64)
                                            ins = e.matmul(PS[px][e_ * 64:(e_ + 1) * 64, h * 64:(h + 1) * 64],
                                                           lhsT=lt[cc][po:po + 64, cs_], rhs=rt[cc][po:po + 64, cs_],
                                                           start=True, stop=True, tile_position=(po, e_ * 64))
                                    return ins
                                return f
                            em_pe_rt(mk(0), mk(1), r=list(blt) + list(brt), w=[b_ps[px]])
                            em("dve", TT(dstt, PS[px], csb(mask, 512), ALU.mult), r=[b_ps[px], b_cstb], w=[bdst])
                        amat(0, BT, AT, b_BT, b_AT, Qa[0], b_Qa[0], "su")
                        yield
                        yield
                        amat(1, AT, BT, b_AT, b_BT, Qt[0], b_Qt[0], "sl")
                        yield
                        yield
                        amat(2, KT, AT, b_KT, b_AT, AakT[pb], b_Aak[pb], "su")
                        yield
                        yield
                        amat(3, BT, RT, b_BT, b_RT, ArbT[pb], b_Arb[pb], "iu")
                        yield
                        yield
                        amat(4, KT, RT, b_KT, b_RT, ArkT[pb], b_Ark[pb], "iu")
                        yield
                        yield
                        em("dve", TT(Pm[0], Qa[0], csb("eye", 512), ALU.add), r=[b_Qa[0], b_cstb], w=[b_Pm[0]])
                        yield

                        def sqmat(px, lt, rt, blt, brt):
                            def mk(e_):
                                def f(e):
                                    es = slice(e_ * 64, (e_ + 1) * 64)
                                    for h in range(8):
                                        hs = slice(h * 64, (h + 1) * 64)
                                        ins = e.matmul(PS[px][es, hs], lhsT=lt[es, hs], rhs=rt[es, hs], start=True, stop=True,
                                                       tile_position=(e_ * 64, e_ * 64))
                                    return ins
                                return f
                            em_pe_rt(mk(0), mk(1), r=[blt, brt], w=[b_ps[px]])
                        cur = 0
                        for lv in range(5):
                            nxt = 1 - cur
                            sqmat(2, Qa[cur], Qt[cur], b_Qa[cur], b_Qt[cur])
                            yield
                            if lv < 4:
                                sqmat(3, Qt[cur], Qa[cur], b_Qt[cur], b_Qa[cur])
                                yield
                            em("act", ACT(Qt[nxt], PS[2], AF.Copy), r=[b_ps[2]], w=[b_Qt[nxt]])
                            yield
                            if lv < 4:
                                em("dve", TCP(Qa[nxt], PS[3]), r=[b_ps[3]], w=[b_Qa[nxt]])
                                yield
                            sqmat(lv % 2, Qt[nxt], Pm[cur], b_Qt[nxt], b_Pm[cur])
                            yield
                            dstP = MT[pb] if lv == 4 else Pm[nxt]
                            bdstP = b_MT[pb] if lv == 4 else b_Pm[nxt]
                            em("dve", TT(dstP, PS[lv % 2], Pm[cur], ALU.add), r=[b_ps[lv % 2], b_Pm[cur]], w=[bdstP])
                            yield
                            cur = nxt
                            yield

                        for e_ in range(2):
                            c = 2 * cp + e_
                            gidx = blk * 8 + c
                            tc_, tn_ = gidx % 2, (gidx + 1) % 2
                            es = slice(e_ * 64, (e_ + 1) * 64)
                            cs_ = slice(c * 64, (c + 1) * 64)

                            def mk_z(first, es=es, cs_=cs_, tc_=tc_, e_=e_):
                                def f(e):
                                    ins = None
                                    if first:
                                        st_ = True
                                        for h in range(8):
                                            cc, par = h // 2, h % 2
                                            hs = slice(h * 64, (h + 1) * 64)
                                            if par == e_:
                                                e.matmul(PS[1][es, hs], lhsT=AT[cc][e_ * 64:e_ * 64 + 64, cs_], rhs=Tbf[tc_][e_ * 64:e_ * 64 + 64, cc, :],
                                                         start=st_, stop=False, tile_position=(e_ * 64, e_ * 64))
                                                st_ = False
                                            ins = e.matmul(PS[1][es, hs], lhsT=AakT[pb][es, hs], rhs=Vtok[pb][es, hs],
                                                           start=st_, stop=False, tile_position=(e_ * 64, e_ * 64))
                                            st_ = False
                                    else:
                                        par = 1 - e_
                                        po = 64 * par
                                        for h in range(par, 8, 2):
                                            cc = h // 2
                                            hs = slice(h * 64, (h + 1) * 64)
                                            ins = e.matmul(PS[1][es, hs], lhsT=AT[cc][po:po + 64, cs_], rhs=Tbf[tc_][po:po + 64, cc, :],
                                                           start=False, stop=(h >= 6), tile_position=(po, e_ * 64))
                                    return ins
                                return f
                            em_pe_rt(mk_z(True), mk_z(False), r=list(b_AT) + [b_Tb[tc_], b_Aak[pb], b_Vtok[pb]], w=[b_ps[1]])
                            yield
                            em("act", ACT(Zb[es, :], PS[1][es, :], AF.Copy), r=[b_ps[1]], w=[b_Zb[e_]])
                            yield
                            yield

                            def f_u(e, es=es, e_=e_):
                                for h in range(8):
                                    hs = slice(h * 64, (h + 1) * 64)
                                    ins = e.matmul(PS[2][es, hs], lhsT=MT[pb][es, hs], rhs=Zb[es, hs], start=True, stop=True,
                                                   tile_position=(e_ * 64, e_ * 64))
                                return ins
                            em("pe", f_u, r=[b_MT[pb], b_Zb[e_]], w=[b_ps[2]])
                            yield
                            em("dve", TCP(Utok[pb][es, :], PS[2][es, :]), r=[b_ps[2]], w=[b_Utok[pb][e_]])
                            yield
                            yield

                            yb = 4 if e_ == 0 else 0

                            def mk_y(first, es=es, cs_=cs_, tc_=tc_, e_=e_, yb=yb):
                                def f(e):
                                    ins = None
                                    if first:
                                        started = [False, False]
                                        for h in range(8):
                                            cc, par = h // 2, h % 2
                                            po = 64 * par
                                            hs = slice(h * 64, (h + 1) * 64)
                                            o_ = PS[yb][po:po + 64, cc * 64:(cc + 1) * 64]
                                            same = (par == e_)
                                            if same:
                                                e.matmul(o_, lhsT=Tbf[tc_][po:po + 64, cc, :], rhs=RT[cc][po:po + 64, cs_],
                                                         start=(not started[par]), stop=False, tile_position=(po, po))
                                                started[par] = True
                                            e.matmul(o_, lhsT=Utok[pb][es, hs], rhs=ArbT[pb][es, hs], start=(not started[par]), stop=False, tile_position=(e_ * 64, po))
                                            started[par] = True
                                            ins = e.matmul(o_, lhsT=Vtok[pb][es, hs], rhs=ArkT[pb][es, hs], start=False, stop=(same and h >= 6), tile_position=(e_ * 64, po))
                                    else:
                                        par = 1 - e_
                                        po = 64 * par
                                        for h in range(par, 8, 2):
                                            cc = h // 2
                                            o_ = PS[yb][po:po + 64, cc * 64:(cc + 1) * 64]
                                            ins = e.matmul(o_, lhsT=Tbf[tc_][po:po + 64, cc, :], rhs=RT[cc][po:po + 64, cs_],
                                                           start=False, stop=(h >= 6), tile_position=(po, po))
                                    return ins
                                return f
                            em_pe_rt(mk_y(True), mk_y(False), r=[b_Tb[tc_], b_Utok[pb][e_], b_Arb[pb], b_Ark[pb], b_Vtok[pb]] + list(b_RT), w=[b_ps[yb]])
                            yield
                            em("act", ACT(ysb[:, :, cs_], PS[yb][:, 0:256].rearrange("p (a b) -> p a b", a=4), AF.Copy), r=[b_ps[yb]], w=[b_ysb] + b_tB)
                            yield

                            def f_t(e, es=es, e_=e_):
                                for h in range(8):
                                    cc, po = h // 2, 64 * (h % 2)
                                    hs = slice(h * 64, (h + 1) * 64)
                                    o_ = PS[3][po:po + 64, cc * 64:(cc + 1) * 64]
                                    e.matmul(o_, lhsT=Btok[pb][es, hs], rhs=Utok[pb][es, hs], start=True, stop=False,
                                             tile_position=(e_ * 64, po))
                                    ins = e.matmul(o_, lhsT=Ktok[pb][es, hs], rhs=Vtok[pb][es, hs], start=False, stop=True,
                                                   tile_position=(e_ * 64, po))
                                return ins
                            em("pe", f_t, r=[b_Btok[pb], b_Utok[pb][e_], b_Ktok[pb], b_Vtok[pb]], w=[b_ps[3]])
                            yield
                            em("dve", TT(Ttmp, PS[3][:, 0:256].rearrange("p (a b) -> p a b", a=4), Tst[tc_], ALU.add),
                               r=[b_ps[3], b_T[tc_]], w=[b_Ttmp], small=True)
                            yield
                            em("dve", TT(Tst[tn_], Ttmp, dWt[:, :, c:c + 1].to_broadcast([128, 4, 64]), ALU.mult),
                               r=[b_Ttmp] + b_dW, w=[b_T[tn_]], small=True)
                            yield
                            em("act", ACT(Tbf[tn_], Tst[tn_], AF.Copy), r=[b_T[tn_]], w=[b_Tb[tn_]])
                            yield
                            yield


                def gen_ssd():
                    wz = [load_w(winv, 1792, 256), load_w(winv, 2048, 256)]
                    for tt4 in range(4):
                        tsl = slice(tt4 * 128, (tt4 + 1) * 128)
                        psi = 5 + tt4 % 2

                        def f_zp(e, tsl=tsl, psi=psi, wz=wz):
                            for half in range(2):
                                for k in range(8):
                                    ins = e.matmul(PS[psi][:, half * 256:(half + 1) * 256], lhsT=hTb[:, k, tsl], rhs=winb[wz[half]][:, k, :],
                                                   start=(k == 0), stop=(k == 7))
                            return ins
                        em("pe", f_zp, r=[b_win[wz[0]], b_win[wz[1]]] + b_hb, w=[b_ps[psi]], cost=2.5)
                        yield
                        em("act", ACT(zs[:, tt4, :], PS[psi], AF.Silu), r=[b_ps[psi]], w=[b_zs[tt4]])
                        yield
                        yield
                    wdt = load_w(winv, 3328, 8)

                    def f_dtp(e, wdt=wdt):
                        for tt4 in range(4):
                            for k in range(8):
                                ins = e.matmul(PS[6][:, tt4 * 8:(tt4 + 1) * 8], lhsT=hTb[:, k, tt4 * 128:(tt4 + 1) * 128], rhs=winb[wdt][:, k, 0:8],
                                               start=(k == 0), stop=(k == 7))
                        return ins
                    em("pe", f_dtp, r=[b_win[wdt]] + b_hb, w=[b_ps[6]])
                    yield
                    em("dve", TT(dtk, PS[6][:, 0:32].rearrange("p (a b) -> p a b", a=4), prr("dtb", 8).unsqueeze(1).to_broadcast([128, 4, 8]), ALU.add),
                       r=[b_ps[6], b_pr], w=[b_dtk], small=True)
                    yield

                    em("act", ACT(dtk, dtk, AF.Exp), r=[b_dtk], w=[b_dtk], small=True)
                    yield
                    em("act", ACT(dtk, dtk, AF.Ln, bias=epsc[:, 3:4]), r=[b_dtk, b_eps], w=[b_dtk], small=True)
                    yield
                    em("dve", TT(atk, dtk, Arow.unsqueeze(1).to_broadcast([128, 4, 8]), ALU.mult), r=[b_dtk, b_Arow], w=[b_atk], small=True)
                    yield
                    em("dve", TCP(ahi, atk), r=[b_atk], w=[b_ahi], small=True)
                    yield
                    em("dve", TT(atmp, atk, ahi, ALU.subtract), r=[b_atk, b_ahi], w=[b_atmp], small=True)
                    yield
                    em("dve", TCP(alo, atmp), r=[b_atmp], w=[b_alo], small=True)
                    yield
                    pctr = 0
                    for jb in range(4):
                        wi = load_w(winv, 2304 + jb * 256, 256)
                        for sub in range(2):
                            c8 = jb * 2 + sub
                            psi = 5 + pctr % 2
                            pctr += 1
                            proj_fm(wi, sub, psi)
                            yield
                            pk = c8 % 2
                            em("act", ACT(pSb[pk][:, 0:3], halo[:, c8, 0:3], AF.Copy), r=[b_halo[c8]], w=[b_pSb[pk]], small=True)
                            yield
                            em("act", ACT(pSb[pk][:, 3:515], PS[psi], AF.Copy), r=[b_ps[psi]], w=[b_pSb[pk]])
                            yield
                            em("act", ACT(halo[:, c8, 0:3], pSb[pk][:, 512:515], AF.Copy), r=[b_pSb[pk]], w=[b_halo[c8]], small=True)
                            yield
                            em("dve", TS(xc, pSb[pk][:, 0:512], ppc("cw0", c8), ppc("cb", c8), ALU.mult, ALU.add), r=[b_pSb[pk], b_pp], w=[b_xc], small=True)
                            yield
                            for wj in range(1, 4):
                                em("dve", STT(xc, pSb[pk][:, wj:wj + 512], ppc("cw%d" % wj, c8), xc, ALU.mult, ALU.add), r=[b_pSb[pk], b_pp, b_xc], w=[b_xc])
                                yield
                            if c8 < 4:
                                em("act", ACT(xsT[:, c8, :], xc, AF.Silu), r=[b_xc], w=[b_xsT[c8]])
                                yield
                            else:
                                em("act", ACT(BCT[:, c8 - 4, :], xc, AF.Silu), r=[b_xc], w=[b_BCT[c8 - 4]])
                                yield
                            yield

                    for cp in range(4):
                        tsl = slice(cp * 128, (cp + 1) * 128)

                        def f_trx(e, tsl=tsl):
                            for cc in range(4):
                                ins = e.transpose(out=PT[0][:, cc * 128:(cc + 1) * 128], in_=xsT[:, cc, tsl], identity=csb("ident", 128))
                            return ins
                        em("pe", f_trx, r=b_xsT + [b_cstb], w=[b_pt[0]])
                        yield
                        em("act", ACT(xtok, PT[0].rearrange("p (a b) -> p a b", a=8), AF.Copy), r=[b_pt[0]], w=[b_xtok])
                        yield

                        def f_trb(e, tsl=tsl):
                            for g_ in range(2):
                                ins = e.transpose(out=PT[1][:, g_ * 128:(g_ + 1) * 128], in_=BCT[:, g_, tsl], identity=csb("ident", 128))
                            return ins
                        em("pe", f_trb, r=b_BCT + [b_cstb], w=[b_pt[1]])
                        yield
                        em("act", ACT(Bk, PT[1][:, 0:256], AF.Copy), r=[b_pt[1]], w=[b_Bk])
                        yield
                        yield
                        dt_bc = dtk[:, cp, :].unsqueeze(2).to_broadcast([128, 8, 64])
                        em("dve", TT(xdt, xtok, dt_bc, ALU.mult), r=[b_xtok, b_dtk], w=[b_xdt])
                        yield
                        a_cp = atk[:, cp, :]

                        def f_cs(e, cp=cp):
                            for i_, k_ in enumerate(("tri", "blk", "sel0", "sel1")):
                                e.matmul(PS[5][:, 8 * i_:8 * i_ + 8], lhsT=csb(k_, 128), rhs=ahi[:, cp, :], start=True, stop=False)
                                ins = e.matmul(PS[5][:, 8 * i_:8 * i_ + 8], lhsT=csb(k_, 128), rhs=alo[:, cp, :], start=False, stop=True)
                            return ins
                        em("pe", f_cs, r=[b_cstb, b_ahi, b_alo], w=[b_ps[5]])
                        yield
                        em("act", ACT(acs, PS[5][:, 0:32], AF.Copy), r=[b_ps[5]], w=[b_acs], small=True)
                        yield
                        em("act", ACT(E32, acs, AF.Exp), r=[b_acs], w=[b_E32], small=True)
                        yield
                        em("dve", TT(dte, acs[:, 8:16], acs[:, 0:8], ALU.subtract), r=[b_acs], w=[b_dte], small=True)
                        yield
                        em("act", ACT(dte, dte, AF.Exp), r=[b_dte], w=[b_dte], small=True)
                        yield
                        em("dve", TT(xdd, xdt, dte.unsqueeze(2).to_broadcast([128, 8, 64]), ALU.mult), r=[b_xdt, b_dte], w=[b_xdd])
                        yield
                        def f_rar(e, cp=cp):
                            e.tensor_tensor(out=rarh, in0=csb("iu", 512).rearrange("p (a b) -> p a b", a=8), in1=ahi[:, cp, :].unsqueeze(2).to_broadcast([128, 8, 64]), op=ALU.mult)
                            return e.tensor_tensor(out=rarl, in0=csb("iu", 512).rearrange("p (a b) -> p a b", a=8), in1=alo[:, cp, :].unsqueeze(2).to_broadcast([128, 8, 64]), op=ALU.mult)
                        em("dve", f_rar, r=[b_cstb, b_ahi, b_alo], w=[b_rar])
                        yield

                        def f_rarm(e):
                            e.matmul(PS[6], lhsT=csb("blk", 128), rhs=rarh.rearrange("p a b -> p (a b)"), start=True, stop=False)
                            return e.matmul(PS[6], lhsT=csb("blk", 128), rhs=rarl.rearrange("p a b -> p (a b)"), start=False, stop=True)
                        em("pe", f_rarm, r=[b_cstb, b_rar], w=[b_ps[6]])
                        yield
                        yield
                        em("dve", TT(seg, PS[6].rearrange("p (a b) -> p a b", a=8), acs[:, 0:8].unsqueeze(2).to_broadcast([128, 8, 64]), ALU.subtract),
                           r=[b_ps[6], b_acs], w=[b_seg])
                        yield
                        em("dve", TS(seg, seg, 0.0, 0.0, ALU.min, ALU.add), r=[b_seg], w=[b_seg])
                        yield
                        em("act", ACT(seg, seg, AF.Exp), r=[b_seg], w=[b_seg])
                        yield

                        def f_sc(e, cp=cp):
                            for e_ in range(2):
                                cs_ = slice((2 * cp + e_) * 64, (2 * cp + e_ + 1) * 64)
                                for g_ in range(2):
                                    ins = e.matmul(PS[5][e_ * 64:(e_ + 1) * 64, g_ * 64:(g_ + 1) * 64], lhsT=BCT[:, g_, cs_], rhs=BCT[:, 2 + g_, cs_],
                                                   start=True, stop=True, tile_position=(0, e_ * 64))
                            return ins
                        em("pe", f_sc, r=b_BCT, w=[b_ps[5]])
                        yield
                        em("dve", TT(scm, PS[5][:, 0:128].rearrange("p (a b) -> p a b", a=2), csb("iu", 128).rearrange("p (a b) -> p a b", a=2), ALU.mult),
                           r=[b_ps[5], b_cstb], w=[b_scm], small=True)
                        yield
                        yield
                        for g_ in range(2):
                            em("dve", TT(Wt[:, 4 * g_:4 * g_ + 4, :], seg[:, 4 * g_:4 * g_ + 4, :], scm[:, g_:g_ + 1, :].to_broadcast([128, 4, 64]), ALU.mult),
                               r=[b_seg, b_scm], w=[b_Wt])
                            yield

                        def mk_yd(e_):
                            def f(e):
                                es = slice(e_ * 64, (e_ + 1) * 64)
                                for h in range(8):
                                    ins = e.matmul(PS[5][es, h * 64:(h + 1) * 64], lhsT=Wt[es, h, :], rhs=xdt[es, h, :], start=True, stop=True,
                                                   tile_position=(e_ * 64, e_ * 64))
                                return ins
                            return f
                        em_pe_rt(mk_yd(0), mk_yd(1), r=[b_Wt, b_xdt], w=[b_ps[5]])
                        yield
                        em("act", ACT(yt, PS[5].rearrange("p (a b) -> p a b", a=8), AF.Copy), r=[b_ps[5]], w=[b_yt])
                        yield
                        yield
                        for e_ in range(2):
                            es = slice(e_ * 64, (e_ + 1) * 64)
                            cs_ = slice((2 * cp + e_) * 64, (2 * cp + e_ + 1) * 64)

                            def f_yo(e, es=es, cs_=cs_, e_=e_):
                                for g_ in range(2):
                                    ins = e.matmul(PS[6][es, g_ * 256:(g_ + 1) * 256], lhsT=BCT[:, 2 + g_, cs_], rhs=sstb[:, g_ * 256:(g_ + 1) * 256],
                                                   start=True, stop=True, tile_position=(0, e_ * 64))
                                return ins
                            em("pe", f_yo, r=b_BCT + [b_sstb], w=[b_ps[6]])
                            yield

                            def f_cst(e, es=es, e_=e_):
                                for h in range(8):
                                    g_ = h // 4
                                    ins = e.matmul(PS[5][:, h * 64:(h + 1) * 64], lhsT=Bk[es, g_ * 128:(g_ + 1) * 128], rhs=xdd[es, h, :],
                                                   start=True, stop=True, tile_position=(e_ * 64, 0))
                                return ins
                            em("pe", f_cst, r=[b_Bk, b_xdd], w=[b_ps[5]])
                            yield
                            cdec = E32[:, 16 + 8 * e_:24 + 8 * e_].unsqueeze(2).to_broadcast([128, 8, 64])
                            em("dve", TT(stmp, sst, cdec, ALU.mult), r=[b_sst, b_E32], w=[b_stmp])
                            yield
                            em("dve", TT(sst, stmp, PS[5].rearrange("p (a b) -> p a b", a=8), ALU.add), r=[b_stmp, b_ps[5]], w=[b_sst])
                            yield
                            em("act", ACT(sstb, sst.rearrange("p a b -> p (a b)"), AF.Copy), r=[b_sst], w=[b_sstb])
                            yield
                            yield
                        em("dve", TT(yt2, PS[6].rearrange("p (a b) -> p a b", a=8), E32[:, 0:8].unsqueeze(2).to_broadcast([128, 8, 64]), ALU.mult),
                           r=[b_ps[6], b_E32], w=[b_yt2])
                        yield
                        em("dve", TT(yt, yt, yt2, ALU.add), r=[b_yt, b_yt2], w=[b_yt])
                        yield
                        em("dve", TT(yt2, xtok, prr("dsk", 8).unsqueeze(2).to_broadcast([128, 8, 64]), ALU.mult), r=[b_xtok, b_pr], w=[b_yt2])
                        yield
                        em("dve", TT(yt, yt, yt2, ALU.add), r=[b_yt, b_yt2], w=[b_yt])
                        yield
                        em("dve", TT(yt, yt, zs[:, cp, :].rearrange("p (a b) -> p a b", a=8), ALU.mult), r=[b_yt, b_zs[cp]], w=[b_yt])
                        yield
                        yield
                        em("act", lambda e: e.activation(out=yt2.rearrange("p a b -> p (a b)"), in_=yt.rearrange("p a b -> p (a b)"), func=AF.Square,
                                                         accum_out=ssq[:, 0:1]), r=[b_yt], w=[b_yt2, b_ssq], small=True)
                        yield

                        em("act", ACT(ssq[:, 1:2], ssq[:, 0:1], AF.Ln, bias=epsc[:, 2:3], scale=1.0 / 512), r=[b_ssq, b_eps], w=[b_ssq], small=True)
                        yield
                        em("act", ACT(ssq[:, 1:2], ssq[:, 1:2], AF.Exp, scale=-0.5), r=[b_ssq], w=[b_ssq], small=True)
                        yield
                        em("dve", STT(ytb, yt.rearrange("p a b -> p (a b)"), ssq[:, 1:2], prr("sng", 512), ALU.mult, ALU.mult),
                           r=[b_yt, b_ssq, b_pr], w=[b_ytb])
                        yield

                        def f_try(e):
                            for cc in range(4):
                                ins = e.transpose(out=PT[0][:, cc * 128:(cc + 1) * 128], in_=ytb[:, cc * 128:(cc + 1) * 128], identity=csb("ident", 128))
                            return ins
                        em("pe", f_try, r=[b_ytb, b_cstb], w=[b_pt[0]])
                        yield
                        em("act", ACT(ycat[:, 4:8, tsl], PT[0].rearrange("p (a b) -> p a b", a=4), AF.Copy), r=[b_pt[0]], w=b_yc[4:8])
                        yield
                        yield


                run_streams({"rw": gen_rwkv(), "ss": gen_ssd()})
                A.release(mP)

                b_q = [None] * 8 + b_qv3
                qv = lambda c, qv3=qv3: qv3[:, c - 8, :]

                def gen_gn(ccs, T4, bT4, g0, bg0, g1, bg1, pb3):
                    t0, t1, t2, t3 = T4
                    bt0, bt1, bt2, bt3 = bT4
                    pA, pB, pC = pb3
                    for cc in ccs:
                        wcs = slice(cc * 128, (cc + 1) * 128)
                        em("act", ACT(g0, ysb[:, cc, :], AF.Copy), r=[b_ysb], w=[bg0])
                        yield
                        em("act", ACT(g1, ysb[:, cc, :], AF.Square), r=[b_ysb], w=[bg1])
                        yield
                        em("pe", lambda e, g0=g0, pA=pA: e.matmul(PS[pA], lhsT=csb("blk", 128), rhs=g0, start=True, stop=True), r=[b_cstb, bg0], w=[b_ps[pA]], cost=0.6)
                        yield
                        em("pe", lambda e, g1=g1, pB=pB: e.matmul(PS[pB], lhsT=csb("blk", 128), rhs=g1, start=True, stop=True), r=[b_cstb, bg1], w=[b_ps[pB]], cost=0.6)
                        yield
                        em("pe", lambda e, cc=cc, pC=pC: e.matmul(PS[pC], lhsT=csb("blk", 128), rhs=rkb[cc], start=True, stop=True), r=[b_cstb, b_rkb[cc]], w=[b_ps[pC]], cost=0.6)
                        yield
                        em("act", ACT(t1, PS[pA], AF.Copy, scale=1.0 / 64), r=[b_ps[pA]], w=[bt1])
                        yield
                        em("dve", TT(t3, t1, t1, ALU.mult), r=[bt1], w=[bt3])
                        yield
                        em("dve", STT(t2, PS[pB], 1.0 / 64, t3, ALU.mult, ALU.subtract), r=[b_ps[pB], bt3], w=[bt2])
                        yield

                        def f_rs2(e, t2=t2):
                            e.activation(out=t2, in_=t2, func=AF.Ln, bias=epsc[:, 1:2])
                            return e.activation(out=t2, in_=t2, func=AF.Exp, scale=-0.5)
                        em("act", f_rs2, r=[bt2, b_eps], w=[bt2], cost=1.2)
                        yield
                        em("dve", TT(t0, ysb[:, cc, :], t1, ALU.subtract), r=[b_ysb, bt1], w=[bt0])
                        yield
                        em("dve", TT(t0, t0, t2, ALU.mult), r=[bt0, bt2], w=[bt0])
                        yield
                        em("dve", TS(t0, t0, ppc("gng", cc), ppc("gnb", cc), ALU.mult, ALU.add), r=[bt0, b_pp], w=[bt0])
                        yield
                        em("dve", TT(t1, PS[pC], qv(8 + cc), ALU.mult), r=[b_ps[pC], b_q[8 + cc], bt0], w=[bt1])
                        yield
                        em("dve", TT(t0, t0, t1, ALU.add), r=[bt0, bt1], w=[bt0])
                        yield
                        em("pe", lambda e, wcs=wcs, pA=pA: e.matmul(PS[pA], lhsT=g2b[:, wcs], rhs=sgd, start=True, stop=True), r=[b_g2b, b_sgd], w=[b_ps[pA]], cost=0.6)
                        yield
                        em("dve", TT(ycat[:, cc, :], PS[pA], t0, ALU.mult), r=[b_ps[pA], bt0], w=[b_yc[cc]])
                        yield
                gsets = []
                for gi in range(2):
                    gsets.append(([A.alloc((128, 512)) for _ in range(4)], [A.buf("gt%d_%d" % (gi, i)) for i in range(4)],
                                  A.alloc((128, 512), BF16), A.buf("gs0_%d" % gi), A.alloc((128, 512), BF16), A.buf("gs1_%d" % gi)))
                run_streams({"g0": gen_gn([0, 2], *gsets[0], (0, 1, 2)), "g1": gen_gn([1, 3], *gsets[1], (4, 5, 6))})
                A.release(mR)

                if CUT < 8:
                    continue
                if DBG:
                    em("pool", lambda e, blk=blk: e.dma_start(out=dbgv[:, :, blk * 512:(blk + 1) * 512], in_=ycat), r=b_yc, w=[b_dbg], dma="ddbg")
                for dq in range(4):
                    wi = load_w(woutv, dq * 256, 256)
                    for sub in range(2):
                        dc = dq * 2 + sub
                        psi = 4 + dc % 2
                        wc = slice(sub * 128, (sub + 1) * 128)

                        def f_o(e, wi=wi, wc=wc, psi=psi):
                            for k in range(8):
                                ins = e.matmul(PS[psi], lhsT=winb[wi][:, k, wc], rhs=ycat[:, k, :], start=(k == 0), stop=(k == 7))
                            return ins
                        em("pe", f_o, r=[b_win[wi]] + b_yc, w=[b_ps[psi]])
                        xs_ = xT[:, dc, blk * 512:(blk + 1) * 512]
                        em("dve", TT(xs_, PS[psi], xs_, ALU.add), r=[b_ps[psi], b_x[dc][blk]], w=[b_x[dc][blk]])
            A.release(m0)

        ffn(0, "g1")
        if stage in ("full", "mix"):
            mixer()
        if stage == "full":
            ffn(1, "g2")
        sq_holder[0] = A.alloc((128, 8, 512), BF16); sq_holder[1] = [A.buf("sqf")]
        ob = [A.alloc((128, 8, 512)) for _ in range(2)]
        b_ob = [[A.buf("ob%d_%d" % (i, c)) for c in range(8)] for i in range(2)]
        b_out = [Buf("out%d" % g) for g in range(4)]
        ov = out_d.rearrange("(c p) t -> p c t", p=128)
        for g in range(4):
            oi = g % 2
            rms_group(g, "gf", lambda c, oi=oi: ob[oi][:, c, :], b_ob[oi], 6)
            em("sp", lambda e, g=g, oi=oi: e.dma_start(out=ov[:, :, g * 512:(g + 1) * 512], in_=ob[oi]),
               r=b_ob[oi], w=[b_out[g]], dma="dout%d" % oi)
        S.wait_all("sp", b_out + [b_dbg])
        S.build(nc, st)
    print("sched: inst=%d waits=%d sems=%d arena_peak=%d" % (S.ninst, S.nwait, len(S.semkeys), A.peak))
    return nc


_CACHE = {}


def kernel(**inputs):
    stage = os.environ.get("MK_STAGE", "full")
    ncores = int(os.environ.get("MK_NCORES", NCORES))
    inp = {k: np.asarray(v) for k, v in inputs.items()}
    pp = _pack_pp(inp)
    pr = _pack_pr(inp)
    cst = _consts()
    x = inp["x"].astype(np.float32, copy=False)
    common = {
        "pp": pp, "pr": pr, "cst": cst,
        "wg1": np.ascontiguousarray(inp["ffn1_w_gate"][0]), "wu1": np.ascontiguousarray(inp["ffn1_w_up"][0]),
        "wd1": np.ascontiguousarray(inp["ffn1_w_down"][0]),
        "wg2": np.ascontiguousarray(inp["ffn2_w_gate"][0]), "wu2": np.ascontiguousarray(inp["ffn2_w_up"][0]),
        "wd2": np.ascontiguousarray(inp["ffn2_w_down"][0]),
        "win": np.ascontiguousarray(inp["w_in"][0]), "wout": np.ascontiguousarray(inp["w_out"][0]),
        "w2": np.ascontiguousarray(inp["rwkv_w2"][0]), "a2": np.ascontiguousarray(inp["rwkv_a2"][0]),
        "g2": np.ascontiguousarray(inp["rwkv_g2"][0]),
    }
    key = (pp.shape[1], pr.shape[1], cst.shape[1], stage)
    nc = build_program(*key)
    in_maps = []
    for b in range(ncores):
        m = dict(common)
        m["xT"] = np.ascontiguousarray(x[b].T)
        in_maps.append(m)
    res = run_bass_kernel_spmd(nc, in_maps, core_ids=list(range(ncores)))
    kernel.last_results = res
    out = np.stack([np.ascontiguousarray(r["outT"].T) for r in res.results], axis=0)
    return out.astype(np.float32, copy=False)
```
